# Optimizing a Trainium2 kernel written in Bass

```python
import jax, jax.numpy as jnp
from jax import lax
import numpy as np

D_MODEL = 2048
BATCH = 4
SEQ = 2048
DEPTH = 1

MEM_LEN = 256
HEAD_DIM = 128
POOL_WINDOWS = (2, 4, 8, 16)
POOL_WIDTH = D_MODEL // 4
POOL_GROUP = POOL_WIDTH // len(POOL_WINDOWS)
ATTN_WIDTH = D_MODEL - POOL_WIDTH
DILATED_PAIRS = ((128, 1), (512, 4), (2048, 16))
ATTN_HEADS = ATTN_WIDTH // HEAD_DIM
HEADS_PER_GROUP = ATTN_HEADS // len(DILATED_PAIRS)
ATTN_OUT_WIDTH = HEADS_PER_GROUP * HEAD_DIM
IN_PROJ_WIDTH = POOL_WIDTH + 3 * ATTN_WIDTH
MIX_OUT_WIDTH = POOL_WIDTH + ATTN_OUT_WIDTH
CROSS_HEADS = 4
CROSS_HEAD_DIM = D_MODEL // CROSS_HEADS
D_FF = 11 * D_MODEL // 4
ROPE_THETA = 10000.0
EPS = 1e-6
BAND_BLOCK = 64
NEG = -1e30

kernel_name = "hybrid_pool_dilated_attn_macaron_block"


def rmsnorm(x, g):
    xf = x.astype(jnp.float32)
    y = xf * lax.rsqrt(jnp.mean(xf * xf, axis=-1, keepdims=True) + EPS)
    return (y * g.astype(jnp.float32)).astype(x.dtype)


def swiglu(x, w_gate, w_up, w_down):
    return (jax.nn.silu(x @ w_gate) * (x @ w_up)) @ w_down


def rope(t, positions):
    half = t.shape[-1] // 2
    inv = ROPE_THETA ** (-jnp.arange(half, dtype=jnp.float32) / half)
    ang = positions.astype(jnp.float32)[..., None] * inv
    cos = jnp.cos(ang)[:, :, None, :]
    sin = jnp.sin(ang)[:, :, None, :]
    t1 = t[..., :half].astype(jnp.float32)
    t2 = t[..., half:].astype(jnp.float32)
    return jnp.concatenate([t1 * cos - t2 * sin, t2 * cos + t1 * sin], axis=-1).astype(t.dtype)


def multiscale_pool(u, pool_w, pool_scale):
    B, S, C = u.shape
    uf = u.astype(jnp.float32)
    cs = jnp.concatenate([jnp.zeros((B, 1, C), jnp.float32), jnp.cumsum(uf, axis=1)], axis=1)
    t = jnp.arange(S)
    means = []
    for gi, w in enumerate(POOL_WINDOWS):
        lo = jnp.clip(t - w // 2, 0, S)
        hi = jnp.clip(t + w - w // 2, 0, S)
        seg = cs[:, :, gi * POOL_GROUP:(gi + 1) * POOL_GROUP]
        cnt = (hi - lo).astype(jnp.float32)[None, :, None]
        means.append((seg[:, hi] - seg[:, lo]) / cnt)
    pooled = (jnp.concatenate(means, axis=-1) - uf).reshape(B, S, len(POOL_WINDOWS), POOL_GROUP)
    y = jnp.einsum('bsgc,gcd->bsgd', pooled, pool_w.astype(jnp.float32)).reshape(B, S, C)
    return (y * pool_scale.astype(jnp.float32)).astype(u.dtype)


def dilated_window_attention(q, k, v, window, dilation):
    B, S, H, Dh = q.shape
    L = S // dilation
    W = (window // 2) // dilation
    nb = -(-L // BAND_BLOCK)
    Lp = nb * BAND_BLOCK

    def to_sub(t):
        return t.reshape(B, L, dilation, H, Dh).transpose(0, 2, 1, 3, 4)

    qb = jnp.pad(to_sub(q), ((0, 0), (0, 0), (0, Lp - L), (0, 0), (0, 0)))
    qb = qb.reshape(B, dilation, nb, BAND_BLOCK, H, Dh)

    def key_blocks(t):
        tp = jnp.pad(to_sub(t), ((0, 0), (0, 0), (BAND_BLOCK, Lp - L + BAND_BLOCK), (0, 0), (0, 0)))
        return jnp.concatenate(
            [tp[:, :, i * BAND_BLOCK:i * BAND_BLOCK + Lp].reshape(B, dilation, nb, BAND_BLOCK, H, Dh)
             for i in range(3)], axis=3)

    kb = key_blocks(k)
    vb = key_blocks(v)
    scores = jnp.einsum('brnqhc,brnkhc->brnhqk', qb, kb,
                        preferred_element_type=jnp.float32) * (Dh ** -0.5)
    qi = jnp.arange(BAND_BLOCK)[:, None]
    kj = jnp.arange(3 * BAND_BLOCK)[None, :]
    band = jnp.abs(kj - BAND_BLOCK - qi) <= W
    kpos = jnp.arange(nb)[:, None] * BAND_BLOCK + jnp.arange(3 * BAND_BLOCK)[None, :] - BAND_BLOCK
    valid = band[None] & ((kpos >= 0) & (kpos < L))[:, None, :]
    scores = jnp.where(valid[None, None, :, None], scores, NEG)
    lse = jax.nn.logsumexp(scores, axis=-1)
    p = jnp.exp(scores - lse[..., None])
    out = jnp.einsum('brnhqk,brnkhc->brnqhc', p.astype(v.dtype), vb)
    out = out.reshape(B, dilation, Lp, H, Dh)[:, :, :L].transpose(0, 2, 1, 3, 4).reshape(B, S, H, Dh)
    lse = lse.transpose(0, 1, 2, 4, 3).reshape(B, dilation, Lp, H)[:, :, :L]
    lse = lse.transpose(0, 2, 1, 3).reshape(B, S, H)
    return out, lse


def hybrid_mixer(u, positions, w_in, pool_w, pool_scale, w_out):
    B, S, _ = u.shape
    z = u @ w_in
    u_pool = z[..., :POOL_WIDTH]
    q = z[..., POOL_WIDTH:POOL_WIDTH + ATTN_WIDTH].reshape(B, S, ATTN_HEADS, HEAD_DIM)
    k = z[..., POOL_WIDTH + ATTN_WIDTH:POOL_WIDTH + 2 * ATTN_WIDTH].reshape(B, S, ATTN_HEADS, HEAD_DIM)
    v = z[..., POOL_WIDTH + 2 * ATTN_WIDTH:].reshape(B, S, ATTN_HEADS, HEAD_DIM)
    q = rope(q, positions)
    k = rope(k, positions)
    outs, lses = [], []
    for g, (window, dil) in enumerate(DILATED_PAIRS):
        sl = slice(g * HEADS_PER_GROUP, (g + 1) * HEADS_PER_GROUP)
        o, l = dilated_window_attention(q[:, :, sl], k[:, :, sl], v[:, :, sl], window, dil)
        outs.append(o)
        lses.append(l)
    outs = jnp.stack(outs, axis=0).astype(jnp.float32)
    wts = jax.nn.softmax(jnp.stack(lses, axis=0), axis=0)
    attn = jnp.sum(wts[..., None] * outs, axis=0).reshape(B, S, ATTN_OUT_WIDTH).astype(u.dtype)
    pool = multiscale_pool(u_pool, pool_w, pool_scale)
    return jnp.concatenate([pool, attn], axis=-1) @ w_out


def memory_cross_attention(u, m, w_cq, w_ck, w_cv, w_co):
    B, S, _ = u.shape
    M = m.shape[1]
    q = (u @ w_cq).reshape(B, S, CROSS_HEADS, CROSS_HEAD_DIM)
    k = (m @ w_ck).reshape(B, M, CROSS_HEADS, CROSS_HEAD_DIM)
    v = (m @ w_cv).reshape(B, M, CROSS_HEADS, CROSS_HEAD_DIM)
    s = jnp.einsum('bshc,bmhc->bhsm', q, k, preferred_element_type=jnp.float32) * (CROSS_HEAD_DIM ** -0.5)
    p = jax.nn.softmax(s, axis=-1)
    o = jnp.einsum('bhsm,bmhc->bshc', p.astype(v.dtype), v).reshape(B, S, CROSS_HEADS * CROSS_HEAD_DIM)
    return o @ w_co


def setup_inputs(seed: int = 0) -> dict:
    key = jax.random.key(seed)
    ks = jax.random.split(key, 24)

    def w(k, shape, fan_in):
        return jax.random.normal(k, shape, jnp.float32) * (fan_in ** -0.5)

    def gain(k, shape):
        return 1.0 + 0.05 * jax.random.normal(k, shape, jnp.float32)

    x = jax.random.normal(ks[0], (BATCH, SEQ, D_MODEL), jnp.float32)
    mem = jax.random.normal(ks[1], (BATCH, MEM_LEN, D_MODEL), jnp.float32)
    offs = jax.random.randint(ks[2], (BATCH, 1), 0, 1024, dtype=jnp.int32)
    positions = (jnp.arange(SEQ, dtype=jnp.int32)[None, :] + offs).astype(jnp.int32)
    L = DEPTH
    return {
        "x": x,
        "mem": mem,
        "positions": positions,
        "ffn1_norm": gain(ks[3], (L, D_MODEL)),
        "ffn1_w_gate": w(ks[4], (L, D_MODEL, D_FF), D_MODEL),
        "ffn1_w_up": w(ks[5], (L, D_MODEL, D_FF), D_MODEL),
        "ffn1_w_down": w(ks[6], (L, D_FF, D_MODEL), D_FF),
        "mix_norm": gain(ks[7], (L, D_MODEL)),
        "w_in": w(ks[8], (L, D_MODEL, IN_PROJ_WIDTH), D_MODEL),
        "pool_w": w(ks[9], (L, len(POOL_WINDOWS), POOL_GROUP, POOL_GROUP), POOL_GROUP),
        "pool_scale": gain(ks[10], (L, POOL_WIDTH)),
        "w_out": w(ks[11], (L, MIX_OUT_WIDTH, D_MODEL), MIX_OUT_WIDTH),
        "cross_norm": gain(ks[12], (L, D_MODEL)),
        "mem_norm": gain(ks[13], (L, D_MODEL)),
        "w_cq": w(ks[14], (L, D_MODEL, D_MODEL), D_MODEL),
        "w_ck": w(ks[15], (L, D_MODEL, D_MODEL), D_MODEL),
        "w_cv": w(ks[16], (L, D_MODEL, D_MODEL), D_MODEL),
        "w_co": w(ks[17], (L, D_MODEL, D_MODEL), D_MODEL),
        "ffn2_norm": gain(ks[18], (L, D_MODEL)),
        "ffn2_w_gate": w(ks[19], (L, D_MODEL, D_FF), D_MODEL),
        "ffn2_w_up": w(ks[20], (L, D_MODEL, D_FF), D_MODEL),
        "ffn2_w_down": w(ks[21], (L, D_FF, D_MODEL), D_FF),
        "final_norm": gain(ks[22], (D_MODEL,)),
    }


def reference(x, mem, positions, ffn1_norm, ffn1_w_gate, ffn1_w_up, ffn1_w_down,
              mix_norm, w_in, pool_w, pool_scale, w_out,
              cross_norm, mem_norm, w_cq, w_ck, w_cv, w_co,
              ffn2_norm, ffn2_w_gate, ffn2_w_up, ffn2_w_down, final_norm):
    h = x
    for layer in range(DEPTH):
        h = h + 0.5 * swiglu(rmsnorm(h, ffn1_norm[layer]),
                             ffn1_w_gate[layer], ffn1_w_up[layer], ffn1_w_down[layer])
        h = h + hybrid_mixer(rmsnorm(h, mix_norm[layer]), positions,
                             w_in[layer], pool_w[layer], pool_scale[layer], w_out[layer])
        h = h + memory_cross_attention(rmsnorm(h, cross_norm[layer]), rmsnorm(mem, mem_norm[layer]),
                                       w_cq[layer], w_ck[layer], w_cv[layer], w_co[layer])
        h = h + 0.5 * swiglu(rmsnorm(h, ffn2_norm[layer]),
                             ffn2_w_gate[layer], ffn2_w_up[layer], ffn2_w_down[layer])
    return rmsnorm(h, final_norm)
```

```python
import numpy as np
from contextlib import ExitStack
import concourse.bass as bass
import concourse.mybir as mybir
from concourse.bass_utils import run_bass_kernel_spmd

F32 = mybir.dt.float32
BF16 = mybir.dt.bfloat16
I32 = mybir.dt.int32
AF = mybir.ActivationFunctionType
ALU = mybir.AluOpType

D = 2048
NT = 1024
KC = 16
NF = 44
DFF = 5632
G = 4
EPS = 1e-6
TWO_PI = 6.283185307179586
C1 = 6.28125
C2 = TWO_PI - C1


class Tr:
    def __init__(self, nc, es):
        self.nc = nc
        self.es = es
        self.ops = []
        self.n = 0
        self.res = {}
        self.engs = {'pe': nc.tensor, 'act': nc.scalar, 'dve': nc.vector, 'pool': nc.gpsimd, 'sp': nc.sync}
        self.last = {}
        self.dmacnt = {}
        self.mil = set()
        self.sem = {e: es.enter_context(nc.semaphore("s_" + e)) for e in self.engs}
        self.dsem = {}
        self.cnt = {e: 0 for e in self.engs}
        self.milval = {}
        self.waited = {e: {} for e in self.engs}
        self.pool_pending = None

    @staticmethod
    def _glob(k):
        return isinstance(k, tuple) and k[0] == 'wfm'

    def op(self, eng, fn, reads=(), writes=(), dk=None):
        idx = self.n
        self.n += 1
        if dk is not None:
            tok = ('D', dk, self.dmacnt.get(dk, 0) + 16)
            self.dmacnt[dk] = tok[2]
        else:
            tok = ('E', eng, idx)
        deps = set()
        for r in reads:
            e = self.res.get(r)
            if e and e[0] is not None:
                deps.add(e[0])
        for w in writes:
            e = self.res.get(w)
            if e:
                if e[0] is not None:
                    deps.add(e[0])
                deps.update(e[1].values())
        for r in reads:
            e = self.res.setdefault(r, [None, {}])
            e[1][tok[:2]] = tok
        for w in writes:
            self.res[w] = [tok, {}]
        d2 = []
        for t in deps:
            if t[0] == 'E':
                if t[1] == eng and eng == 'pe' and dk is None:
                    continue
                self.mil.add(t[2])
            d2.append(t)
        if eng == 'pool' and self.pool_pending is not None and not all(self._glob(w) for w in writes):
            d2 = d2 + self.pool_pending
            self.pool_pending = None
        self.ops.append((idx, eng, fn, d2, dk))
        if dk is None:
            self.last[eng] = idx

    def barrier(self, final=False):
        toks = [('E', e, i) for e, i in self.last.items()] + [('D', k, v) for k, v in self.dmacnt.items()]
        for t in toks:
            if t[0] == 'E':
                self.mil.add(t[2])
        for e in self.engs:
            dl = [t for t in toks if not (t[0] == 'E' and t[1] == e)]
            if e == 'pool' and not final:
                self.pool_pending = dl
                continue
            self.ops.append((-1, e, None, dl, None))
        self.res = {k: v for k, v in self.res.items() if self._glob(k)}
        self.flush()

    def flush(self):
        nc = self.nc
        for (idx, eng, fn, deps, dk) in self.ops:
            E = self.engs[eng]
            for t in deps:
                if t[0] == 'E':
                    sm = self.sem[t[1]]
                    v = self.milval[t[2]]
                else:
                    if t[1] not in self.dsem:
                        self.dsem[t[1]] = self.es.enter_context(nc.semaphore("d%d" % len(self.dsem)))
                    sm = self.dsem[t[1]]
                    v = t[2]
                if self.waited[eng].get(t[:2], 0) >= v:
                    continue
                E.wait_ge(sm, v)
                self.waited[eng][t[:2]] = v
            if fn is None:
                continue
            ins = fn()
            if dk is not None:
                if dk not in self.dsem:
                    self.dsem[dk] = self.es.enter_context(nc.semaphore("d%d" % len(self.dsem)))
                ins.then_inc(self.dsem[dk], 16)
            elif idx in self.mil:
                self.cnt[eng] += 1
                self.milval[idx] = self.cnt[eng]
                ins.then_inc(self.sem[eng], 1)
        self.ops = []


def build_nc():
    nc = bass.Bass("TRN2", target_bir_lowering=False)
    ses = ExitStack()
    T = Tr(nc, ses)

    def din(name, shape, dt=F32):
        return nc.dram_tensor(name, list(shape), dt, kind="ExternalInput").ap()

    xT_own = din("xT_own", [D, NT])
    xT_oth = din("xT_oth", [D, NT])
    memT = din("memT", [D, 256])
    pos_own = din("pos_own", [128, NT], I32)
    pos_oth = din("pos_oth", [128, NT], I32)
    cst = din("cst", [128, 64])
    edge = din("edge", [128, 64])
    gains = din("gains", [128, 6 * KC])
    pscale = din("pscale", [128, 4])
    masks = din("masks", [128, 5 * 384])
    ident_in = din("ident", [128, 128])
    swap_in = din("swapm", [128, 128])
    w1gu = din("w1gu", [2 * NF, 128, KC * 128])
    w1d = din("w1d", [DFF, D])
    w2gu = din("w2gu", [2 * NF, 128, KC * 128])
    w2d = din("w2d", [DFF, D])
    win1 = din("win1", [28, 128, KC * 128])
    win2 = din("win2", [40, 128, KC * 128])
    poolw = din("poolw", [4, 128, 128])
    wout = din("wout", [16, 128, 8 * 128])
    wcq = din("wcq", [16, 128, KC * 128])
    wck = din("wck", [16, 128, KC * 128])
    wcv = din("wcv", [16, 128, KC * 128])
    wco = din("wco", [64, 128, 4 * 128])
    outT = nc.dram_tensor("outT", [D, NT], F32, kind="ExternalOutput").ap()
    scr_k = nc.dram_tensor("scr_k", [12, 128, NT], BF16).ap()
    scr_v = nc.dram_tensor("scr_v", [12, 128, 16, 128], BF16).ap()

    es = ExitStack()

    uid = [0]

    def sb(name, shape, dt, stack=None):
        uid[0] += 1
        return (stack or es).enter_context(nc.sbuf_tensor("%s_%d" % (name, uid[0]), list(shape), dt))

    HB = sb("HB", [128, KC, NT], F32)
    XN = sb("XN", [128, KC, NT], BF16)
    NWFM = 5
    WFM = [sb("wfm%d" % i, [128, KC, 128], BF16) for i in range(NWFM)]
    CST = sb("CST", [128, 64], F32)
    EDGE = sb("EDGE", [128, 64], F32)
    GN = sb("GN", [128, 6 * KC], F32)
    PSC = sb("PSC", [128, 4], F32)
    MSK = sb("MSK", [128, 5 * 384], BF16)
    IDB = sb("IDB", [128, 128], BF16)
    SWP = sb("SWP", [128, 128], F32)
    ONF = sb("ONF", [128, 128], F32)
    ONB = sb("ONB", [128, 128], BF16)
    EPSB = sb("EPSB", [128, 1], F32)
    UPH = sb("UPH", [128, 4, 16], F32)
    SQG = [sb("sqg%d" % i, [128, 512], F32) for i in range(3)]
    RS = sb("RS", [128, NT], F32)
    COS2 = sb("COS2", [128, NT], F32)
    SIN2 = sb("SIN2", [128, NT], F32)
    es0 = ExitStack()
    MSKF = sb("MSKF", [128, 5 * 384], F32, es0)
    IDF = sb("IDF", [128, 128], F32, es0)
    PS = [es.enter_context(nc.psum_tensor("ps%d" % i, [128, 512], F32)) for i in range(8)]
    PSB = PS[7][:].bitcast(BF16)
    PSBK = 7

    bank_rr = {'A': [0, 1], 'B': [2, 3], 'C': [4, 5], 'M': [6], 'W': [2, 3, 4, 5]}
    bank_i = {k: 0 for k in bank_rr}

    def bank(cls):
        b = bank_rr[cls][bank_i[cls] % len(bank_rr[cls])]
        bank_i[cls] += 1
        return b

    wfm_i = [0]

    def dve(fn, reads, writes):
        T.op('dve', fn, reads, writes)

    def act(fn, reads, writes):
        T.op('act', fn, reads, writes)

    def mm(out, lhsT, rhs, start, stop, reads, bk):
        T.op('pe', lambda: nc.tensor.matmul(out, lhsT, rhs, start=start, stop=stop), reads, [('ps', bk)])

    def load(eng, out, in_, key, writes, reads=()):
        E = nc.sync if eng == 'sp' else nc.gpsimd
        T.op(eng, lambda: E.dma_start(out=out, in_=in_), reads, writes, dk=key)

    HKEYS = [('h', 'main', c, hf) for c in range(KC) for hf in range(2)]
    XKEYS = [('xn', 'main', 0), ('xn', 'main', 1)]

    for (dst, src, nm) in [(CST, cst, 'c0'), (EDGE, edge, 'c1'), (GN, gains, 'c2'), (PSC, pscale, 'c3'),
                           (MSKF, masks, 'c4'), (IDF, ident_in, 'c5'), (SWP, swap_in, 'c6')]:
        load('sp', dst[:], src, ('k', nm), [nm])
    dve(lambda: nc.vector.memset(ONF[:], 1.0), [], ['onf'])
    dve(lambda: nc.vector.memset(ONB[:], 1.0), [], ['onb'])
    dve(lambda: nc.vector.memset(EPSB[:], EPS), [], ['eps'])
    dve(lambda: nc.vector.tensor_copy(out=MSK[:], in_=MSKF[:]), ['c4'], ['msk'])
    dve(lambda: nc.vector.tensor_copy(out=IDB[:], in_=IDF[:]), ['c5'], ['idb'])
    T.barrier()
    es0.close()

    sqi = [0]

    def stat_acc(src, c, hf, w, tag, banks, delay):
        i = sqi[0] % 3
        sqi[0] += 1
        sl = slice(hf * 512, hf * 512 + w)
        act(lambda: nc.scalar.activation(out=SQG[i][:, 0:w], in_=src[:, c, sl], func=AF.Square),
            [('h', tag, c, hf)], [('sqg', i)])
        bk = banks[hf]

        def st2():
            mm(PS[bk][:, 0:w], ONF[:], SQG[i][:, 0:w], c == 0, c == KC - 1, [('sqg', i)], bk)
        if delay:
            defer(st2, delay)
        else:
            st2()

    def norm_stats(src, ntok, tag, banks):
        nh = (ntok + 511) // 512
        w = min(ntok, 512)
        for hf in range(nh):
            for c in range(KC):
                stat_acc(src, c, hf, w, tag, banks, 0)

    def norm_apply(src, ntok, gcol, dst, tag, banks, store=False):
        nh = (ntok + 511) // 512
        w = min(ntok, 512)
        for hf in range(nh):
            sl = slice(hf * 512, hf * 512 + w)
            bk = banks[hf]
            act(lambda sl=sl, bk=bk: nc.scalar.activation(out=RS[:, sl], in_=PS[bk][:, 0:w], func=AF.Sqrt,
                                                          bias=EPSB[:], scale=1.0 / D),
                [('ps', bk)], [('rs', hf)])
            dve(lambda sl=sl: nc.vector.reciprocal(out=RS[:, sl], in_=RS[:, sl]), [('rs', hf)], [('rs', hf)])
        tot = nh * w
        for c in range(KC):
            hk = [('h', tag, c, hf) for hf in range(nh)]
            dve(lambda c=c: nc.vector.scalar_tensor_tensor(
                out=dst[:, c, 0:tot], in0=src[:, c, 0:tot], scalar=GN[:, gcol * KC + c:gcol * KC + c + 1],
                in1=RS[:, 0:tot], op0=ALU.mult, op1=ALU.mult),
                hk + [('rs', hf) for hf in range(nh)],
                [('xn', tag, hf, c) for hf in range(nh)] + (hk if dst is src else []))
            if store and c % 4 == 3:
                ov = outT.rearrange("(c p) t -> p c t", p=128)
                c0 = c - 3
                load('sp', ov[:, c0:c + 1, :], dst[:, c0:c + 1, :], ('k', 'out', c0), [('out', c0)],
                     reads=[('xn', tag, hf, cc) for hf in range(nh) for cc in range(c0, c + 1)]
                     + [('h', tag, cc, hf) for hf in range(nh) for cc in range(c0, c + 1)])

    deferred = []
    bg = []
    bg_low = []
    bg_rate = [1]

    def defer(fn, delay=1):
        deferred.append([delay, fn])

    def hook():
        run = []
        keep = []
        for e in deferred:
            e[0] -= 1
            (run if e[0] <= 0 else keep).append(e)
        deferred[:] = keep
        for e in run:
            e[1]()
        for _ in range(bg_rate[0]):
            if bg:
                bg.pop(0)()
        if bg_low:
            bg_low.pop(0)()

    def drain_all():
        while deferred or bg or bg_low:
            run = deferred[:]
            del deferred[:]
            for e in run:
                e[1]()
            if bg:
                bg.pop(0)()
            elif bg_low:
                bg_low.pop(0)()

    def xnk(tag='main'):
        return lambda hf: [('xn', tag, hf, c) for c in range(KC)]

    def proj_fm(Wr, cb, nk, rhs_fn, widths, evac_fn, rhs_reads, cls='A'):
        slot = wfm_i[0] % NWFM
        wfm_i[0] += 1
        load('pool', WFM[slot][:, 0:nk, :], Wr[cb].rearrange("p (k c) -> p k c", c=128)[:, 0:nk, :], ('wfm', slot), [('wfm', slot)])
        for hf, w in enumerate(widths):
            bk = bank(cls)
            rr = rhs_reads(hf) if callable(rhs_reads) else rhs_reads
            for k in range(nk):
                mm(PS[bk][:, 0:w], WFM[slot][:, k, :], rhs_fn(k, hf), k == 0, k == nk - 1,
                   [('wfm', slot)] + rr, bk)
            hook()
            evac_fn(hf, bk)

    def ffn(Wgu, Wd, gcol, banks_in, full_stats):
        with ExitStack() as st:
            if full_stats:
                norm_stats(HB, NT, 'main', banks_in)
            norm_apply(HB, NT, gcol, XN, 'main', banks_in)
            ACTB = [sb("actb%d" % i, [128, NT], BF16, st) for i in range(2 * G)]
            WD = [sb("wd%d" % i, [128, D], BF16, st) for i in range(2 * G)]
            SG = [sb("sg%d" % i, [128, 512], F32, st) for i in range(2)]
            sgi = [0]
            for gi in range(NF // G):
                for j in range(G):
                    f = gi * G + j
                    s = (gi % 2) * G + j
                    load('pool', WD[s][:], Wd[f * 128:(f + 1) * 128, :], ('wd', s), [('wd', s)])
                for j in range(G):
                    f = gi * G + j
                    s = (gi % 2) * G + j
                    gbanks = {}

                    def ev_gate(hf, bk):
                        gbanks[hf] = bk

                    proj_fm(Wgu, 2 * f, KC, lambda k, hf: XN[:, k, hf * 512:(hf + 1) * 512], [512, 512],
                            ev_gate, xnk(), 'A')

                    def ev_up(hf, bk, s=s):
                        q = SG[sgi[0] % 2]
                        qi = sgi[0] % 2
                        sgi[0] += 1
                        gb = gbanks[hf]
                        act(lambda: nc.scalar.activation(out=q[:], in_=PS[gb][:], func=AF.Silu),
                            [('ps', gb)], [('sg', qi)])
                        dve(lambda: nc.vector.tensor_tensor(out=ACTB[s][:, hf * 512:(hf + 1) * 512], in0=q[:],
                                                            in1=PS[bk][:], op=ALU.mult),
                            [('sg', qi), ('ps', bk)], [('actb', s)])

                    proj_fm(Wgu, 2 * f + 1, KC, lambda k, hf: XN[:, k, hf * 512:(hf + 1) * 512], [512, 512],
                            ev_up, xnk(), 'B')
                for oc in range(KC):
                    for hf in range(2):
                        bk = bank('C')
                        for j in range(G):
                            s = (gi % 2) * G + j
                            mm(PS[bk][:], WD[s][:, oc * 128:(oc + 1) * 128], ACTB[s][:, hf * 512:(hf + 1) * 512],
                               j == 0, j == G - 1, [('wd', s), ('actb', s)], bk)
                        sl = slice(hf * 512, (hf + 1) * 512)
                        dve(lambda oc=oc, sl=sl, bk=bk: nc.vector.scalar_tensor_tensor(
                            out=HB[:, oc, sl], in0=PS[bk][:], scalar=0.5, in1=HB[:, oc, sl],
                            op0=ALU.mult, op1=ALU.add),
                            [('ps', bk), ('h', 'main', oc, hf)], [('h', 'main', oc, hf)])
                        hook()
                        if gi == NF // G - 1:
                            stat_acc(HB, oc, hf, 512, 'main', (6, 7), 2)
            drain_all()
            T.barrier()

    def cs_temps(st):
        return (sb("posi", [128, NT], I32, st), sb("ang", [128, NT], F32, st), sb("kf", [128, NT], F32, st),
                sb("ki", [128, NT], I32, st), sb("mk", [128, NT], F32, st))

    def make_cs(pos_ap, COS, SIN, temps, tag):
        PI_, A, Kf, Ki, Mk = temps
        load('sp', PI_[:], pos_ap, ('k', 'pos' + tag), ['posi'])
        dve(lambda: nc.vector.tensor_copy(out=A[:], in_=PI_[:]), ['posi'], ['ang'])
        dve(lambda: nc.vector.tensor_scalar(out=A[:], in0=A[:], scalar1=CST[:, 0:1], scalar2=None, op0=ALU.mult),
            ['ang'], ['ang'])
        for which, R in ((0, SIN), (1, COS)):
            off = 0.0 if which == 0 else TWO_PI / 4
            rk = ('cs', tag, which)
            dve(lambda off=off: nc.vector.tensor_scalar(out=Kf[:], in0=A[:], scalar1=off, scalar2=1.0 / TWO_PI,
                                                        op0=ALU.add, op1=ALU.mult), ['ang'], ['kf'])
            dve(lambda: nc.vector.tensor_copy(out=Ki[:], in_=Kf[:]), ['kf'], ['ki'])
            dve(lambda: nc.vector.tensor_copy(out=Kf[:], in_=Ki[:]), ['ki'], ['kf'])
            dve(lambda off=off, R=R: nc.vector.tensor_scalar(out=R[:], in0=A[:], scalar1=off, scalar2=None, op0=ALU.add),
                ['ang'], [rk])
            dve(lambda R=R: nc.vector.scalar_tensor_tensor(out=R[:], in0=Kf[:], scalar=-C1, in1=R[:],
                                                           op0=ALU.mult, op1=ALU.add), ['kf', rk], [rk])
            dve(lambda R=R: nc.vector.scalar_tensor_tensor(out=R[:], in0=Kf[:], scalar=-C2, in1=R[:],
                                                           op0=ALU.mult, op1=ALU.add), ['kf', rk], [rk])
            dve(lambda R=R: nc.vector.tensor_scalar(out=Mk[:], in0=R[:], scalar1=3.141592, scalar2=None, op0=ALU.is_gt),
                [rk], ['mk'])
            dve(lambda R=R: nc.vector.scalar_tensor_tensor(out=R[:], in0=Mk[:], scalar=-TWO_PI, in1=R[:],
                                                           op0=ALU.mult, op1=ALU.add), ['mk', rk], [rk])
            dve(lambda R=R: nc.vector.tensor_scalar(out=Mk[:], in0=R[:], scalar1=-3.141592, scalar2=None, op0=ALU.is_lt),
                [rk], ['mk'])
            dve(lambda R=R: nc.vector.scalar_tensor_tensor(out=R[:], in0=Mk[:], scalar=TWO_PI, in1=R[:],
                                                           op0=ALU.mult, op1=ALU.add), ['mk', rk], [rk])
            dve(lambda R=R: nc.vector.tensor_scalar(out=R[:], in0=R[:], scalar1=3.1415925, scalar2=-3.1415925,
                                                    op0=ALU.min, op1=ALU.max), [rk], [rk])

        def fin():
            for which, R in ((0, SIN), (1, COS)):
                rk = ('cs', tag, which)
                act(lambda R=R: nc.scalar.activation(out=R[:], in_=R[:], func=AF.Sin), [rk], [rk])
            dve(lambda: nc.vector.tensor_scalar(out=SIN[:], in0=SIN[:], scalar1=CST[:, 1:2], scalar2=None, op0=ALU.mult),
                [('cs', tag, 0)], [('cs', tag, 0)])
        return fin

    def make_rope(COS, SIN, st, tag):
        QF = [sb("qf%s%d" % (tag, i), [128, 512], F32, st) for i in range(2)]
        T1 = [sb("t1%s" % tag, [128, 512], F32, st)] * 2
        T2 = [sb("t2%s" % tag, [128, 512], F32, st)] * 2
        ri = [0]

        def rope(bk, hf, dst, dst_key):
            i = ri[0] % 2
            ri[0] += 1
            sl = slice(hf * 512, (hf + 1) * 512)
            act(lambda: nc.scalar.copy(out=QF[i][:], in_=PS[bk][:]), [('ps', bk)], [('qf', i)])

            def stage2():
                b2 = bank('M')
                mm(PS[b2][:], SWP[:], QF[i][:], True, True, [('qf', i)], b2)
                dve(lambda: nc.vector.tensor_tensor(out=T1[i][:], in0=QF[i][:], in1=COS[:, sl], op=ALU.mult),
                    [('qf', i), ('cs', tag, 1)], [('t1', 0)])
                dve(lambda: nc.vector.tensor_tensor(out=T2[i][:], in0=PS[b2][:], in1=SIN[:, sl], op=ALU.mult),
                    [('ps', b2), ('cs', tag, 0)], [('t2', 0)])
                dve(lambda: nc.vector.tensor_tensor(out=dst, in0=T1[i][:], in1=T2[i][:], op=ALU.add),
                    [('t1', 0), ('t2', 0)], [dst_key])
            defer(stage2, 1)
        return rope

    def tile_ap(buf2d, g, t):
        if g == 0:
            return buf2d[:, t * 128:(t + 1) * 128]
        if g == 1:
            r, lb = t // 2, t % 2
            return buf2d[:, 512 * lb + r:512 * lb + 512:4]
        return buf2d[:, t:NT:16]

    def v_tiles(VT, g, dst, dst_key, vt_key):
        PSB3 = PSB[:, :].rearrange("p (t d) -> p t d", d=128)
        vk = list(vt_key) if isinstance(vt_key, list) else [vt_key]
        if g < 2:
            for t in range(8):
                T.op('pe', lambda t=t: nc.tensor.transpose(PSB[:, t * 128:(t + 1) * 128], tile_ap(VT, g, t), IDB[:]),
                     vk + ['idb'], [('ps', PSBK)])
            act(lambda: nc.scalar.copy(out=dst[:, 0:8, :], in_=PSB3), [('ps', PSBK)], [dst_key])
        else:
            for rb in range(2):
                for r8 in range(8):
                    T.op('pe', lambda r8=r8, rb=rb: nc.tensor.transpose(PSB[0:64, r8 * 128:(r8 + 1) * 128],
                                                                    tile_ap(VT, 2, rb * 8 + r8), IDB[:]),
                         vk + ['idb'], [('ps', PSBK)])
                act(lambda rb=rb: nc.scalar.copy(out=dst[0:64, rb * 8:(rb + 1) * 8, :], in_=PSB3[0:64]),
                    [('ps', PSBK)], [dst_key])

    def load_x(src, nm):
        v = src.rearrange("(c p) t -> p c t", p=128)
        for hf in range(2):
            load('sp', HB[:, :, hf * 512:(hf + 1) * 512], v[:, :, hf * 512:(hf + 1) * 512], ('k', nm, hf),
                 [('h', 'main', c, hf) for c in range(KC)])

    load_x(xT_oth, 'xo')
    ffn(w1gu, w1d, 0, (6, 7), True)
    with ExitStack() as st:
        norm_apply(HB, NT, 1, XN, 'main', (6, 7))
        load_x(xT_own, 'xw')
        COS = sb("cos1", [128, NT], F32, st)
        SIN = sb("sin1", [128, NT], F32, st)
        temps = cs_temps(st)
        cs_fin = [make_cs(pos_oth, COS, SIN, temps, 'o'), make_cs(pos_own, COS2, SIN2, temps, 'w')]
        rope = make_rope(COS, SIN, st, 'o')
        KST = [sb("kst%d" % i, [128, NT], BF16, st) for i in range(2)]
        VTS = [sb("vts%d" % i, [128, NT], BF16, st) for i in range(2)]
        VST = [sb("vst%d" % i, [128, 16, 128], BF16, st) for i in range(2)]
        xr = lambda k, hf: XN[:, k, hf * 512:(hf + 1) * 512]
        bg_rate[0] = 1
        for hd in range(12):
            g = hd // 4
            i = hd % 2

            def ev_v(hf, bk, i=i):
                act(lambda: nc.scalar.copy(out=VTS[i][:, hf * 512:(hf + 1) * 512], in_=PS[bk][:]),
                    [('ps', bk)], [('vts', i, hf)])
            proj_fm(win1, 2 * hd + 1, KC, xr, [512, 512], ev_v, xnk(), 'B')

            def vt_bg(hd=hd, g=g, i=i):
                v_tiles(VTS[i][:], g, VST[i], ('vst', i), [('vts', i, 0), ('vts', i, 1)])
                if g < 2:
                    load('sp', scr_v[hd][:, 0:8, :], VST[i][:, 0:8, :], ('st', 'v', i), [('scrv', hd)], reads=[('vst', i)])
                else:
                    load('sp', scr_v[hd][0:64], VST[i][0:64], ('st', 'v', i), [('scrv', hd)], reads=[('vst', i)])
            bg.append(vt_bg)
            if hd == 6:
                for f in cs_fin:
                    f()
            if hd == 5:
                for hf in range(2):
                    for c in range(KC):
                        bg_low.append(lambda c=c, hf=hf: stat_acc(HB, c, hf, 512, 'main', (4, 5), 1))
        for hd in range(12):
            i = hd % 2
            proj_fm(win1, 2 * hd, KC, xr, [512, 512],
                    lambda hf, bk, i=i: rope(bk, hf, KST[i][:, hf * 512:(hf + 1) * 512], ('kst', i, hf)),
                    xnk(), 'A')
            defer(lambda hd=hd, i=i: load('sp', scr_k[hd], KST[i][:], ('st', 'k', i), [('scrk', hd)],
                                          reads=[('kst', i, 0), ('kst', i, 1)]), 2)
        for gpl in range(4):
            def ev_h(hf, bk, gpl=gpl):
                dve(lambda: nc.vector.tensor_scalar(out=UPH[:, gpl, hf * 8:(hf + 1) * 8], in0=PS[bk][:, 0:8],
                                                    scalar1=CST[:, 2 + hf:3 + hf], scalar2=None, op0=ALU.mult),
                    [('ps', bk)], [('uph', gpl, hf)])
            proj_fm(win1, 24 + gpl, KC,
                    lambda k, hf: XN[:, k, 1016:1024] if hf == 0 else XN[:, k, 0:8], [8, 8], ev_h,
                    lambda hf: [('xn', 'main', 1 - hf, c) for c in range(KC)], 'A')
        drain_all()
        T.barrier()

    ffn(w1gu, w1d, 0, (4, 5), False)
    with ExitStack() as st:
        norm_apply(HB, NT, 1, XN, 'main', (6, 7))
        CCa = sb("cca", [128, 4, NT], BF16, st)
        with ExitStack() as sta:
            rope = make_rope(COS2, SIN2, sta, 'w')
            QG = sb("qg", [128, 3, NT], BF16, sta)
            KG = sb("kg", [128, 3, 2 * NT], BF16, sta)
            VG = sb("vg", [128, 3, 16, 128], BF16, sta)
            VTS = sb("vts2", [128, NT], BF16, sta)
            NUM = sb("num", [128, NT], F32, sta)
            DEN = sb("den", [128, NT], F32, sta)
            PT = [sb("pt%d" % i, [128, 512], BF16, sta) for i in range(2)]
            PE_ = [sb("pe%d" % i, [128, 512], BF16, sta) for i in range(2)]
            xr = lambda k, hf: XN[:, k, hf * 512:(hf + 1) * 512]
            pti = [0]
            SC = 128 ** -0.5
            NDK = lambda nm, blks: [(nm, bb) for bb in blks]

            def att_closures(j, g):
                tiles = list(range(8 if g < 2 else 2))
                info = {}

                def A(t):
                    if g == 0:
                        kts = [(t - 1) % 16, t, t + 1]
                        kaps = [KG[:, g, kt * 128:(kt + 1) * 128] for kt in kts]
                        vaps = [VG[:, g, kt if kt < 8 else (8 if kt == 8 else 15), :] for kt in kts]
                        mcol = 384 if t == 0 else (768 if t == 7 else 0)
                        blks = [t]
                    elif g == 1:
                        r, lb = t // 2, t % 2
                        lts = [(lb - 1) % 4, lb, lb + 1]
                        kaps = [KG[:, g, 512 * lt + r:512 * lt + 512:4] for lt in lts]
                        vaps = [VG[:, g, (r * 2 + lt) if lt < 2 else (8 + r * 2 + (lt - 2)), :] for lt in lts]
                        mcol = 384 if lb == 0 else 768
                        blks = [4 * lb + x for x in range(4)]
                    else:
                        kaps = [KG[:, g, (t * 8 + r8):2 * NT:16] for r8 in range(8)]
                        vaps = [VG[:, g, t * 8 + r8, :] for r8 in range(8)]
                        mcol = 1152
                        blks = list(range(8))
                    nk = len(kaps)
                    qw = 128 if g < 2 else 64
                    w = nk * qw
                    bk = bank('C')
                    for ki, kap in enumerate(kaps):
                        qap = tile_ap(QG[:, g, :], g, t) if g < 2 else tile_ap(QG[:, g, :], 2, t * 8 + ki)
                        mm(PS[bk][:, ki * qw:(ki + 1) * qw], kap, qap, True, True,
                           [('kg', g, 0), ('kg', g, 1), ('kgo', g), ('qg', g, 0), ('qg', g, 1)], bk)
                    pi = pti[0] % 2
                    pti[0] += 1
                    act(lambda: nc.scalar.activation(out=PE_[pi][:, 0:w], in_=PS[bk][:, 0:w], func=AF.Exp, scale=SC),
                        [('ps', bk)], [('pe', pi)])
                    dve(lambda: nc.vector.tensor_tensor(out=PT[pi][:, 0:w], in0=PE_[pi][:, 0:w],
                                                        in1=MSK[:, mcol:mcol + w], op=ALU.mult),
                        [('pe', pi), 'msk'], [('pt', pi)])
                    info[t] = (pi, nk, vaps, blks)

                def B(t):
                    pi, nk, vaps, blks = info[t]
                    bo, bd = 2, 3
                    if g < 2:
                        for ki, vap in enumerate(vaps):
                            mm(PS[bo][:, 0:128], vap, PT[pi][:, ki * 128:(ki + 1) * 128], ki == 0, ki == nk - 1,
                               [('vg', g), ('vgo', g), ('pt', pi)], bo)
                        for ki in range(nk):
                            mm(PS[bd][:, 0:128], ONB[:], PT[pi][:, ki * 128:(ki + 1) * 128], ki == 0, ki == nk - 1,
                               [('pt', pi), 'onb'], bd)
                        nap = tile_ap(NUM[:], g, t)
                        dap = tile_ap(DEN[:], g, t)
                        pso = PS[bo][:, 0:128]
                        psd = PS[bd][:, 0:128]
                    else:
                        for ki, vap in enumerate(vaps):
                            mm(PS[bo][:, ki * 64:(ki + 1) * 64], vap, PT[pi][:, ki * 64:(ki + 1) * 64], True, True,
                               [('vg', g), ('vgo', g), ('pt', pi)], bo)
                        for ki in range(nk):
                            mm(PS[bd][:, ki * 64:(ki + 1) * 64], ONB[:], PT[pi][:, ki * 64:(ki + 1) * 64], True, True,
                               [('pt', pi), 'onb'], bd)
                        nap = NUM[:].rearrange("p (l s) -> p l s", s=16)[:, :, t * 8:(t + 1) * 8]
                        dap = DEN[:].rearrange("p (l s) -> p l s", s=16)[:, :, t * 8:(t + 1) * 8]
                        pso = PS[bo][:, :].rearrange("p (s l) -> p l s", s=8)
                        psd = PS[bd][:, :].rearrange("p (s l) -> p l s", s=8)
                    nk_, dk_ = NDK('num', blks), NDK('den', blks)
                    if g == 0:
                        act(lambda: nc.scalar.copy(out=nap, in_=pso), [('ps', bo)], nk_)
                        dve(lambda: nc.vector.tensor_copy(out=dap, in_=psd), [('ps', bd)], dk_)
                    else:
                        dve(lambda: nc.vector.tensor_tensor(out=nap, in0=pso, in1=nap, op=ALU.add),
                            [('ps', bo)] + nk_, nk_)
                        dve(lambda: nc.vector.tensor_tensor(out=dap, in0=psd, in1=dap, op=ALU.add),
                            [('ps', bd)] + dk_, dk_)

                cl = []
                for idx, t in enumerate(tiles):
                    cl.append(lambda t=t: A(t))
                    if idx >= 1:
                        cl.append(lambda t=tiles[idx - 1]: B(t))
                cl.append(lambda t=tiles[-1]: B(t))
                if g == 2:
                    def fin():
                        allb = list(range(8))
                        dve(lambda: nc.vector.reciprocal(out=DEN[:], in_=DEN[:]), NDK('den', allb), NDK('den', allb))
                        dve(lambda: nc.vector.tensor_tensor(out=CCa[:, j, :], in0=NUM[:], in1=DEN[:], op=ALU.mult),
                            NDK('num', allb) + NDK('den', allb), ['cca'] + NDK('num', allb) + NDK('den', allb))
                    cl.append(fin)
                return cl

            for j in range(4):
                for g in range(3):
                    hd = 4 * g + j
                    base = (j * 3 + g) * 3
                    bg_rate[0] = max(1, (len(bg) + 5) // 6)
                    proj_fm(win2, base, KC, xr, [512, 512],
                            lambda hf, bk, g=g: rope(bk, hf, QG[:, g, hf * 512:(hf + 1) * 512], ('qg', g, hf)),
                            xnk(), 'A')
                    proj_fm(win2, base + 1, KC, xr, [512, 512],
                            lambda hf, bk, g=g: rope(bk, hf, KG[:, g, hf * 512:(hf + 1) * 512], ('kg', g, hf)),
                            xnk(), 'A')
                    load('sp', KG[:, g, NT:2 * NT], scr_k[hd], ('kgo', g), [('kgo', g)], reads=[('scrk', hd)])

                    def ev_v(hf, bk):
                        act(lambda: nc.scalar.copy(out=VTS[:, hf * 512:(hf + 1) * 512], in_=PS[bk][:]),
                            [('ps', bk)], [('vts2', hf)])
                    proj_fm(win2, base + 2, KC, xr, [512, 512], ev_v, xnk(), 'A')
                    drain_all()
                    v_tiles(VTS[:], g, VG[:, g], ('vg', g), [('vts2', 0), ('vts2', 1)])
                    if g < 2:
                        load('sp', VG[:, g, 8:16, :], scr_v[hd][:, 0:8, :], ('vgo', g), [('vgo', g)], reads=[('scrv', hd)])
                    else:
                        load('sp', VG[64:128, g, :, :], scr_v[hd][0:64], ('vgo', g), [('vgo', g)], reads=[('scrv', hd)])
                    bg.extend(att_closures(j, g))
            drain_all()
            T.barrier()
        with ExitStack() as stp:
            bg_rate[0] = 1
            CCp = sb("ccp", [128, 4, NT], BF16, stp)
            UPs = [sb("up%d" % i, [128, NT + 16], F32, stp) for i in range(2)]
            A1 = sb("a1", [128, NT + 16], F32, stp)
            B4 = sb("b4", [128, NT + 16], F32, stp)
            PW = sb("pw", [128, 4, 128], BF16, stp)
            PLs = [sb("pl%d" % i, [128, NT], BF16, stp) for i in range(2)]
            load('pool', PW[:], poolw.rearrange("g c d -> c g d"), ('k', 'pw'), ['pw'])
            xr = lambda k, hf: XN[:, k, hf * 512:(hf + 1) * 512]
            W_ = NT + 16
            for gpl in range(4):
                wlen = (2, 4, 8, 16)[gpl]
                UP = UPs[gpl % 2]
                PL = PLs[gpl % 2]
                upk = ('up', gpl % 2)
                plk = ('pl', gpl % 2)

                def ev_u(hf, bk, UP=UP, upk=upk):
                    act(lambda: nc.scalar.copy(out=UP[:, 8 + hf * 512:8 + (hf + 1) * 512], in_=PS[bk][:]),
                        [('ps', bk)], [upk])
                proj_fm(win2, 36 + gpl, KC, xr, [512, 512], ev_u, xnk(), 'A')
                dve(lambda gpl=gpl, UP=UP: nc.vector.tensor_copy(out=UP[:, 0:8], in_=UPH[:, gpl, 0:8]), [], [upk])
                dve(lambda gpl=gpl, UP=UP: nc.vector.tensor_copy(out=UP[:, NT + 8:NT + 16], in_=UPH[:, gpl, 8:16]),
                    [], [upk])
                dve(lambda UP=UP: nc.vector.tensor_tensor(out=A1[:, 1:W_], in0=UP[:, 1:W_], in1=UP[:, 0:W_ - 1],
                                                          op=ALU.add), [upk], ['a1'])
                cur, curk = A1, 'a1'
                if gpl >= 1:
                    dve(lambda: nc.vector.tensor_tensor(out=B4[:, 2:W_ - 1], in0=A1[:, 1:W_ - 2], in1=A1[:, 3:W_],
                                                        op=ALU.add), ['a1'], ['b4'])
                    cur, curk = B4, 'b4'
                if gpl >= 2:
                    dve(lambda: nc.vector.tensor_tensor(out=A1[:, 4:W_ - 3], in0=B4[:, 2:W_ - 5], in1=B4[:, 6:W_ - 1],
                                                        op=ALU.add), ['b4'], ['a1'])
                    cur, curk = A1, 'a1'
                if gpl >= 3:
                    dve(lambda: nc.vector.tensor_tensor(out=B4[:, 8:W_ - 7], in0=A1[:, 4:W_ - 11], in1=A1[:, 12:W_ - 3],
                                                        op=ALU.add), ['a1'], ['b4'])
                    cur, curk = B4, 'b4'
                dve(lambda cur=cur, gpl=gpl: nc.vector.tensor_tensor(out=cur[:, 8:16], in0=cur[:, 8:16],
                                                                     in1=EDGE[:, gpl * 16:gpl * 16 + 8], op=ALU.mult),
                    [curk], [curk])
                dve(lambda cur=cur, gpl=gpl: nc.vector.tensor_tensor(out=cur[:, NT:NT + 8], in0=cur[:, NT:NT + 8],
                                                                     in1=EDGE[:, gpl * 16 + 8:gpl * 16 + 16], op=ALU.mult),
                    [curk], [curk])
                dve(lambda cur=cur, wlen=wlen, UP=UP, PL=PL: nc.vector.scalar_tensor_tensor(
                    out=PL[:], in0=cur[:, 8:NT + 8], scalar=1.0 / wlen, in1=UP[:, 8:NT + 8],
                    op0=ALU.mult, op1=ALU.subtract), [curk, upk], [plk])
                for hf in range(2):
                    bk = bank('B')
                    mm(PS[bk][:], PW[:, gpl, :], PL[:, hf * 512:(hf + 1) * 512], True, True, ['pw', plk], bk)
                    act(lambda gpl=gpl, hf=hf, bk=bk: nc.scalar.activation(
                        out=CCp[:, gpl, hf * 512:(hf + 1) * 512], in_=PS[bk][:], func=AF.Copy,
                        scale=PSC[:, gpl:gpl + 1]), [('ps', bk)], ['ccp'])
            for oc in range(KC):
                def ev_o(hf, bk, oc=oc):
                    sl = slice(hf * 512, (hf + 1) * 512)
                    dve(lambda: nc.vector.tensor_tensor(out=HB[:, oc, sl], in0=PS[bk][:], in1=HB[:, oc, sl], op=ALU.add),
                        [('ps', bk), ('h', 'main', oc, hf)], [('h', 'main', oc, hf)])
                    stat_acc(HB, oc, hf, 512, 'main', (6, 7), 2)
                proj_fm(wout, oc, 8, lambda k, hf: (CCp[:, k, hf * 512:(hf + 1) * 512] if k < 4
                                                    else CCa[:, k - 4, hf * 512:(hf + 1) * 512]),
                        [512, 512], ev_o, ['ccp', 'cca'], 'W')
            drain_all()
            T.barrier()
    T.barrier()

    with ExitStack() as st:
        MN = sb("mn", [128, KC, 256], BF16, st)
        KCm = sb("kcm", [128, KC, 256], BF16, st)
        VCm = sb("vcm", [128, 2, D], BF16, st)
        with ExitStack() as st2:
            MF = sb("mf", [128, KC, 256], F32, st2)
            load('sp', MF[:], memT.rearrange("(c p) t -> p c t", p=128), ('k', 'mem'),
                 [('h', 'mem', c, 0) for c in range(KC)])
            norm_stats(MF, 256, 'mem', (4, 4))
            norm_apply(MF, 256, 3, MN, 'mem', (4, 4))
            T.barrier()
        norm_apply(HB, NT, 2, XN, 'main', (6, 7))
        bg_rate[0] = 1
        mr = lambda k, hf: MN[:, k, :]
        VT2 = [sb("vt2%d" % i, [128, 256], BF16, st) for i in range(2)]
        for cb in range(KC):
            proj_fm(wck, cb, KC, mr, [256],
                    lambda hf, bk, cb=cb: act(lambda: nc.scalar.copy(out=KCm[:, cb, :], in_=PS[bk][:, 0:256]),
                                              [('ps', bk)], ['kcm']), [('xn', 'mem', 0, c) for c in range(KC)], 'A')
        for cb in range(KC):
            i = cb % 2

            def ev_cv(hf, bk, i=i, cb=cb):
                act(lambda: nc.scalar.copy(out=VT2[i][:], in_=PS[bk][:, 0:256]), [('ps', bk)], [('vt2', i)])

                def st2_():
                    for mt in range(2):
                        T.op('pe', lambda mt=mt: nc.tensor.transpose(PSB[:, mt * 128:(mt + 1) * 128],
                                                                   VT2[i][:, mt * 128:(mt + 1) * 128], IDB[:]),
                             [('vt2', i), 'idb'], [('ps', PSBK)])
                    act(lambda: nc.scalar.copy(out=VCm[:, :, cb * 128:(cb + 1) * 128],
                                               in_=PSB[:, 0:256].rearrange("p (m d) -> p m d", d=128)),
                        [('ps', PSBK)], ['vcm'])
                defer(st2_, 1)
            proj_fm(wcv, cb, KC, mr, [256], ev_cv, [('xn', 'mem', 0, c) for c in range(KC)], 'B')
        drain_all()
        QC = [sb("qc%d" % i, [128, 4, NT], BF16, st) for i in range(2)]
        OCh = [sb("och%d" % i, [128, 4, NT], BF16, st) for i in range(2)]
        PTc = [sb("ptc%d" % i, [128, 2, 512], BF16, st) for i in range(2)]
        RD = [sb("rd%d" % i, [128, 512], F32, st) for i in range(2)]
        xr = lambda k, hf: XN[:, k, hf * 512:(hf + 1) * 512]
        SCc = 512 ** -0.5
        ci = [0]

        def catt_closures(hh):
            qb = QC[hh % 2]
            ob = OCh[hh % 2]
            info = {}

            def S1(hf):
                sl = slice(hf * 512, (hf + 1) * 512)
                pi = ci[0] % 2
                ci[0] += 1
                for mt in range(2):
                    bk = bank('C')
                    for dc in range(4):
                        mm(PS[bk][:], KCm[:, 4 * hh + dc, mt * 128:(mt + 1) * 128], qb[:, dc, sl], dc == 0, dc == 3,
                           ['kcm', ('qc', hh % 2, hf)], bk)
                    act(lambda bk=bk, mt=mt: nc.scalar.activation(out=PTc[pi][:, mt, :], in_=PS[bk][:],
                                                                  func=AF.Exp, scale=SCc),
                        [('ps', bk)], [('ptc', pi)])
                info[hf] = pi

            def S2(hf):
                sl = slice(hf * 512, (hf + 1) * 512)
                pi = info[hf]
                bd = bank('M')
                for mt in range(2):
                    mm(PS[bd][:], ONB[:], PTc[pi][:, mt, :], mt == 0, mt == 1, [('ptc', pi), 'onb'], bd)
                dve(lambda: nc.vector.reciprocal(out=RD[pi][:], in_=PS[bd][:]), [('ps', bd)], [('rd', pi)])
                for dc in range(4):
                    bo = bank('B')
                    for mt in range(2):
                        mm(PS[bo][:], VCm[:, mt, (4 * hh + dc) * 128:(4 * hh + dc + 1) * 128], PTc[pi][:, mt, :],
                           mt == 0, mt == 1, ['vcm', ('ptc', pi)], bo)
                    dve(lambda dc=dc, bo=bo: nc.vector.tensor_tensor(out=ob[:, dc, sl], in0=PS[bo][:], in1=RD[pi][:],
                                                                     op=ALU.mult),
                        [('ps', bo), ('rd', pi)], [('och', hh % 2, hf)])
            return [lambda: S1(0), lambda: S1(1), lambda: S2(0), lambda: S2(1)]

        for hh in range(5):
            if hh >= 1:
                bg.extend(catt_closures(hh - 1))
            bg_rate[0] = 1
            if hh < 4:
                for dc in range(4):
                    proj_fm(wcq, 4 * hh + dc, KC, xr, [512, 512],
                            lambda hf, bk, dc=dc, hh=hh: act(
                                lambda: nc.scalar.copy(out=QC[hh % 2][:, dc, hf * 512:(hf + 1) * 512], in_=PS[bk][:]),
                                [('ps', bk)], [('qc', hh % 2, hf)]), xnk(), 'A')
            drain_all()
            if hh >= 1:
                h0 = hh - 1
                for oc in range(KC):
                    def ev_co(hf, bk, oc=oc):
                        sl = slice(hf * 512, (hf + 1) * 512)
                        dve(lambda: nc.vector.tensor_tensor(out=HB[:, oc, sl], in0=PS[bk][:], in1=HB[:, oc, sl], op=ALU.add),
                            [('ps', bk), ('h', 'main', oc, hf)], [('h', 'main', oc, hf)])
                        if h0 == 3:
                            stat_acc(HB, oc, hf, 512, 'main', (6, 7), 2)
                    proj_fm(wco, h0 * 16 + oc, 4, lambda k, hf, h0=h0: OCh[h0 % 2][:, k, hf * 512:(hf + 1) * 512],
                            [512, 512], ev_co, lambda hf, h0=h0: [('och', h0 % 2, hf)], 'W')
        drain_all()
        T.barrier()

    ffn(w2gu, w2d, 4, (6, 7), False)
    norm_apply(HB, NT, 5, HB, 'main', (6, 7), store=True)
    T.barrier(final=True)

    es.close()
    ses.close()
    return nc


def _fm(W, nk):
    K, C = W.shape
    nb = C // 128
    return np.ascontiguousarray(W.reshape(nk, 128, nb, 128).transpose(2, 1, 0, 3)).reshape(nb, 128, nk * 128)


_NC_CACHE = {}


def kernel(x, mem, positions, ffn1_norm, ffn1_w_gate, ffn1_w_up, ffn1_w_down, mix_norm, w_in, pool_w, pool_scale,
           w_out, cross_norm, mem_norm, w_cq, w_ck, w_cv, w_co, ffn2_norm, ffn2_w_gate, ffn2_w_up, ffn2_w_down,
           final_norm):
    f32 = np.float32
    x = np.asarray(x, f32)
    mem = np.asarray(mem, f32)
    positions = np.asarray(positions, np.int32)

    def gu(wg, wu):
        a = _fm(np.asarray(wg, f32)[0], KC)
        b = _fm(np.asarray(wu, f32)[0], KC)
        return np.ascontiguousarray(np.stack([a, b], axis=1).reshape(2 * NF, 128, KC * 128))

    w1gu = gu(ffn1_w_gate, ffn1_w_up)
    w2gu = gu(ffn2_w_gate, ffn2_w_up)
    w1d = np.ascontiguousarray(np.asarray(ffn1_w_down, f32)[0])
    w2d = np.ascontiguousarray(np.asarray(ffn2_w_down, f32)[0])
    winf = _fm(np.asarray(w_in, f32)[0], KC)
    i1 = []
    for hd in range(12):
        i1 += [16 + hd, 28 + hd]
    i1 += [0, 1, 2, 3]
    i2 = []
    for j in range(4):
        for g in range(3):
            hd = 4 * g + j
            i2 += [4 + hd, 16 + hd, 28 + hd]
    i2 += [0, 1, 2, 3]
    win1 = np.ascontiguousarray(winf[i1])
    win2 = np.ascontiguousarray(winf[i2])
    woutf = _fm(np.asarray(w_out, f32)[0], 8)
    wcqf = _fm(np.asarray(w_cq, f32)[0], KC)
    wckf = _fm(np.asarray(w_ck, f32)[0], KC)
    wcvf = _fm(np.asarray(w_cv, f32)[0], KC)
    wco0 = np.asarray(w_co, f32)[0]
    wcof = np.zeros((64, 128, 4 * 128), f32)
    for hh in range(4):
        blk = _fm(wco0[hh * 512:(hh + 1) * 512, :], 4)
        wcof[hh * 16:(hh + 1) * 16, :, 0:512] = blk
    poolw = np.ascontiguousarray(np.asarray(pool_w, f32)[0])

    def pc(v):
        return np.asarray(v, f32).reshape(KC, 128).T

    gains = np.ascontiguousarray(np.concatenate(
        [pc(ffn1_norm[0]), pc(mix_norm[0]), pc(cross_norm[0]), pc(mem_norm[0]), pc(ffn2_norm[0]), pc(final_norm)],
        axis=1))
    pscale = np.ascontiguousarray(np.asarray(pool_scale, f32)[0].reshape(4, 128).T)
    ident = np.eye(128, dtype=f32)
    swapm = np.zeros((128, 128), f32)
    for m in range(128):
        swapm[(m + 64) % 128, m] = 1.0
    ii = np.arange(128)[:, None]
    jj = np.arange(128)[None, :]
    m_prev = (ii >= jj + 64).astype(f32)
    m_own = (np.abs(ii - jj) <= 64).astype(f32)
    m_next = (ii <= jj - 64).astype(f32)
    z = np.zeros((128, 128), f32)
    li, ei = ii // 2, ii % 2
    lj, ej = jj // 2, jj % 2
    same = (ei == ej)
    invf = (10000.0 ** (-(np.arange(64, dtype=f32)) / 64.0)).astype(f32)

    in_maps = []
    for c in range(8):
        b, half = c // 2, c % 2
        own = slice(half * NT, (half + 1) * NT)
        oth = slice((1 - half) * NT, (2 - half) * NT)
        cst = np.zeros((128, 64), f32)
        cst[:, 0] = np.concatenate([invf, invf])
        cst[:64, 1] = -1.0
        cst[64:, 1] = 1.0
        cst[:, 2] = 1.0 if half == 1 else 0.0
        cst[:, 3] = 1.0 if half == 0 else 0.0
        edge = np.ones((128, 64), f32)
        for gi, w in enumerate((2, 4, 8, 16)):
            t = np.arange(NT) + half * NT
            lo = np.clip(t - w // 2, 0, 2048)
            hi = np.clip(t + w - w // 2, 0, 2048)
            corr = (w / (hi - lo)).astype(f32)
            edge[:, gi * 16:gi * 16 + 8] = corr[None, 0:8]
            edge[:, gi * 16 + 8:gi * 16 + 16] = corr[None, NT - 8:NT]
        m01 = np.concatenate([m_prev, m_own, m_next], axis=1)
        m01_first = np.concatenate([m_prev if half == 1 else z, m_own, m_next], axis=1)
        m01_last = np.concatenate([m_prev, m_own, m_next if half == 0 else z], axis=1)
        l2i = np.arange(128)[:, None]
        l2j = np.arange(64)[None, :]
        if half == 0:
            m2s = ((l2i < 64) | ((l2i - 64) <= l2j)).astype(f32)
        else:
            m2s = ((l2i < 64) | ((l2i - 64) >= l2j)).astype(f32)
        m2 = np.tile(m2s, (1, 8))
        masks = np.ascontiguousarray(np.concatenate([m01, m01_first, m01_last, m2, np.zeros((128, 256), f32)], axis=1))
        in_maps.append({
            "xT_own": np.ascontiguousarray(x[b, own, :].T),
            "xT_oth": np.ascontiguousarray(x[b, oth, :].T),
            "memT": np.ascontiguousarray(mem[b].T),
            "pos_own": np.ascontiguousarray(np.broadcast_to(positions[b, own][None, :], (128, NT))),
            "pos_oth": np.ascontiguousarray(np.broadcast_to(positions[b, oth][None, :], (128, NT))),
            "cst": cst, "edge": edge, "gains": gains, "pscale": pscale, "masks": masks,
            "ident": ident, "swapm": swapm,
            "w1gu": w1gu, "w1d": w1d, "w2gu": w2gu, "w2d": w2d, "win1": win1, "win2": win2,
            "poolw": poolw, "wout": woutf, "wcq": wcqf, "wck": wckf, "wcv": wcvf, "wco": wcof,
        })
    if 'nc' not in _NC_CACHE:
        _NC_CACHE['nc'] = build_nc()
    nc = _NC_CACHE['nc']
    res = run_bass_kernel_spmd(nc, in_maps, core_ids=list(range(8)))
    out = np.empty((4, 2048, D), f32)
    for c in range(8):
        b, half = c // 2, c % 2
        out[b, half * NT:(half + 1) * NT, :] = res.results[c]["outT"].T
    return out
```

```python
import numpy as np
from contextlib import ExitStack
import concourse.bass as bass
import concourse.mybir as mybir
from concourse.bass_utils import run_bass_kernel_spmd

F32 = mybir.dt.float32
BF16 = mybir.dt.bfloat16
I32 = mybir.dt.int32
AF = mybir.ActivationFunctionType
ALU = mybir.AluOpType

D = 2048
NT = 1024
KC = 16
NF = 44
DFF = 5632
G = 4
EPS = 1e-6
TWO_PI = 6.283185307179586
C1 = 6.28125
C2 = TWO_PI - C1


class Tr:
    def __init__(self, nc, es):
        self.nc = nc
        self.es = es
        self.ops = []
        self.n = 0
        self.res = {}
        self.engs = {'pe': nc.tensor, 'act': nc.scalar, 'dve': nc.vector, 'pool': nc.gpsimd, 'sp': nc.sync}
        self.last = {}
        self.dmacnt = {}
        self.mil = set()
        self.sem = {e: es.enter_context(nc.semaphore("s_" + e)) for e in self.engs}
        self.dsem = {}
        self.cnt = {e: 0 for e in self.engs}
        self.milval = {}
        self.waited = {e: {} for e in self.engs}
        self.pool_pending = None

    @staticmethod
    def _glob(k):
        return isinstance(k, tuple) and k[0] == 'wfm'

    def op(self, eng, fn, reads=(), writes=(), dk=None):
        idx = self.n
        self.n += 1
        if dk is not None:
            tok = ('D', dk, self.dmacnt.get(dk, 0) + 16)
            self.dmacnt[dk] = tok[2]
        else:
            tok = ('E', eng, idx)
        deps = set()
        for r in reads:
            e = self.res.get(r)
            if e and e[0] is not None:
                deps.add(e[0])
        for w in writes:
            e = self.res.get(w)
            if e:
                if e[0] is not None:
                    deps.add(e[0])
                deps.update(e[1].values())
        for r in reads:
            e = self.res.setdefault(r, [None, {}])
            e[1][tok[:2]] = tok
        for w in writes:
            self.res[w] = [tok, {}]
        d2 = []
        for t in deps:
            if t[0] == 'E':
                if t[1] == eng and eng == 'pe' and dk is None:
                    continue
                self.mil.add(t[2])
            d2.append(t)
        if eng == 'pool' and self.pool_pending is not None and not all(self._glob(w) for w in writes):
            d2 = d2 + self.pool_pending
            self.pool_pending = None
        self.ops.append((idx, eng, fn, d2, dk))
        if dk is None:
            self.last[eng] = idx

    def barrier(self, final=False):
        toks = [('E', e, i) for e, i in self.last.items()] + [('D', k, v) for k, v in self.dmacnt.items()]
        for t in toks:
            if t[0] == 'E':
                self.mil.add(t[2])
        for e in self.engs:
            dl = [t for t in toks if not (t[0] == 'E' and t[1] == e)]
            if e == 'pool' and not final:
                self.pool_pending = dl
                continue
            self.ops.append((-1, e, None, dl, None))
        self.res = {k: v for k, v in self.res.items() if self._glob(k)}
        self.flush()

    def flush(self):
        nc = self.nc
        for (idx, eng, fn, deps, dk) in self.ops:
            E = self.engs[eng]
            for t in deps:
                if t[0] == 'E':
                    sm = self.sem[t[1]]
                    v = self.milval[t[2]]
                else:
                    if t[1] not in self.dsem:
                        self.dsem[t[1]] = self.es.enter_context(nc.semaphore("d%d" % len(self.dsem)))
                    sm = self.dsem[t[1]]
                    v = t[2]
                if self.waited[eng].get(t[:2], 0) >= v:
                    continue
                E.wait_ge(sm, v)
                self.waited[eng][t[:2]] = v
            if fn is None:
                continue
            ins = fn()
            if dk is not None:
                if dk not in self.dsem:
                    self.dsem[dk] = self.es.enter_context(nc.semaphore("d%d" % len(self.dsem)))
                ins.then_inc(self.dsem[dk], 16)
            elif idx in self.mil:
                self.cnt[eng] += 1
                self.milval[idx] = self.cnt[eng]
                ins.then_inc(self.sem[eng], 1)
        self.ops = []


def build_nc():
    nc = bass.Bass("TRN2", target_bir_lowering=False)
    ses = ExitStack()
    T = Tr(nc, ses)

    def din(name, shape, dt=F32):
        return nc.dram_tensor(name, list(shape), dt, kind="ExternalInput").ap()

    xT_own = din("xT_own", [D, NT])
    xT_oth = din("xT_oth", [D, NT])
    memT = din("memT", [D, 256])
    pos_own = din("pos_own", [128, NT], I32)
    pos_oth = din("pos_oth", [128, NT], I32)
    cst = din("cst", [128, 64])
    edge = din("edge", [128, 64])
    gains = din("gains", [128, 6 * KC])
    pscale = din("pscale", [128, 4])
    masks = din("masks", [128, 5 * 384])
    ident_in = din("ident", [128, 128])
    swap_in = din("swapm", [128, 128])
    w1gu = din("w1gu", [2 * NF, 128, KC * 128])
    w1d = din("w1d", [DFF, D])
    w2gu = din("w2gu", [2 * NF, 128, KC * 128])
    w2d = din("w2d", [DFF, D])
    win1 = din("win1", [28, 128, KC * 128])
    win2 = din("win2", [40, 128, KC * 128])
    poolw = din("poolw", [4, 128, 128])
    wout = din("wout", [16, 128, 8 * 128])
    wcq = din("wcq", [16, 128, KC * 128])
    wck = din("wck", [16, 128, KC * 128])
    wcv = din("wcv", [16, 128, KC * 128])
    wco = din("wco", [32, 128, 8 * 128])
    outT = nc.dram_tensor("outT", [D, NT], F32, kind="ExternalOutput").ap()
    scr_k = nc.dram_tensor("scr_k", [12, 128, NT], BF16).ap()
    scr_v = nc.dram_tensor("scr_v", [12, 128, 16, 128], BF16).ap()

    es = ExitStack()

    uid = [0]

    def sb(name, shape, dt, stack=None):
        uid[0] += 1
        return (stack or es).enter_context(nc.sbuf_tensor("%s_%d" % (name, uid[0]), list(shape), dt))

    HB = sb("HB", [128, KC, NT], F32)
    XN = sb("XN", [128, KC, NT], BF16)
    NWFM = 5
    WFM = [sb("wfm%d" % i, [128, KC, 128], BF16) for i in range(NWFM)]
    CST = sb("CST", [128, 64], F32)
    EDGE = sb("EDGE", [128, 64], F32)
    GN = sb("GN", [128, 6 * KC], F32)
    PSC = sb("PSC", [128, 4], F32)
    MSK = sb("MSK", [128, 5 * 384], BF16)
    IDB = sb("IDB", [128, 128], BF16)
    SWP = sb("SWP", [128, 128], F32)
    ONF = sb("ONF", [128, 128], F32)
    ONB = sb("ONB", [128, 128], BF16)
    EPSB = sb("EPSB", [128, 1], F32)
    UPH = sb("UPH", [128, 4, 16], F32)
    SQG = [sb("sqg%d" % i, [128, 512], F32) for i in range(3)]
    RS = sb("RS", [128, NT], F32)
    COS2 = sb("COS2", [128, NT], F32)
    SIN2 = sb("SIN2", [128, NT], F32)
    es0 = ExitStack()
    MSKF = sb("MSKF", [128, 5 * 384], F32, es0)
    IDF = sb("IDF", [128, 128], F32, es0)
    PS = [es.enter_context(nc.psum_tensor("ps%d" % i, [128, 512], F32)) for i in range(8)]
    PSB = PS[7][:].bitcast(BF16)
    PSBK = 7

    bank_rr = {'A': [0, 1], 'B': [2, 3], 'C': [4, 5], 'M': [6], 'W': [2, 3, 4, 5]}
    bank_i = {k: 0 for k in bank_rr}

    def bank(cls):
        b = bank_rr[cls][bank_i[cls] % len(bank_rr[cls])]
        bank_i[cls] += 1
        return b

    wfm_i = [0]

    def dve(fn, reads, writes):
        T.op('dve', fn, reads, writes)

    def act(fn, reads, writes):
        T.op('act', fn, reads, writes)

    def mm(out, lhsT, rhs, start, stop, reads, bk):
        T.op('pe', lambda: nc.tensor.matmul(out, lhsT, rhs, start=start, stop=stop), reads, [('ps', bk)])

    def load(eng, out, in_, key, writes, reads=()):
        E = nc.sync if eng == 'sp' else nc.gpsimd
        T.op(eng, lambda: E.dma_start(out=out, in_=in_), reads, writes, dk=key)

    HKEYS = [('h', 'main', c, hf) for c in range(KC) for hf in range(2)]
    XKEYS = [('xn', 'main', 0), ('xn', 'main', 1)]

    for (dst, src, nm) in [(CST, cst, 'c0'), (EDGE, edge, 'c1'), (GN, gains, 'c2'), (PSC, pscale, 'c3'),
                           (MSKF, masks, 'c4'), (IDF, ident_in, 'c5'), (SWP, swap_in, 'c6')]:
        load('sp', dst[:], src, ('k', nm), [nm])
    dve(lambda: nc.vector.memset(ONF[:], 1.0), [], ['onf'])
    dve(lambda: nc.vector.memset(ONB[:], 1.0), [], ['onb'])
    dve(lambda: nc.vector.memset(EPSB[:], EPS), [], ['eps'])
    dve(lambda: nc.vector.tensor_copy(out=MSK[:], in_=MSKF[:]), ['c4'], ['msk'])
    dve(lambda: nc.vector.tensor_copy(out=IDB[:], in_=IDF[:]), ['c5'], ['idb'])
    T.barrier()
    es0.close()

    sqi = [0]

    def stat_acc(src, c, hf, w, tag, banks, delay):
        i = sqi[0] % 3
        sqi[0] += 1
        sl = slice(hf * 512, hf * 512 + w)
        act(lambda: nc.scalar.activation(out=SQG[i][:, 0:w], in_=src[:, c, sl], func=AF.Square),
            [('h', tag, c, hf)], [('sqg', i)])
        bk = banks[hf]

        def st2():
            mm(PS[bk][:, 0:w], ONF[:], SQG[i][:, 0:w], c == 0, c == KC - 1, [('sqg', i)], bk)
        if delay:
            defer(st2, delay)
        else:
            st2()

    def norm_stats(src, ntok, tag, banks):
        nh = (ntok + 511) // 512
        w = min(ntok, 512)
        for c in range(KC):
            for hf in range(nh):
                stat_acc(src, c, hf, w, tag, banks, 0)

    def norm_apply(src, ntok, gcol, dst, tag, banks, store=False):
        nh = (ntok + 511) // 512
        w = min(ntok, 512)
        for hf in range(nh):
            sl = slice(hf * 512, hf * 512 + w)
            bk = banks[hf]
            act(lambda sl=sl, bk=bk: nc.scalar.activation(out=RS[:, sl], in_=PS[bk][:, 0:w], func=AF.Sqrt,
                                                          bias=EPSB[:], scale=1.0 / D),
                [('ps', bk)], [('rs', hf)])
            dve(lambda sl=sl: nc.vector.reciprocal(out=RS[:, sl], in_=RS[:, sl]), [('rs', hf)], [('rs', hf)])
        tot = nh * w
        for c in range(KC):
            hk = [('h', tag, c, hf) for hf in range(nh)]
            dve(lambda c=c: nc.vector.scalar_tensor_tensor(
                out=dst[:, c, 0:tot], in0=src[:, c, 0:tot], scalar=GN[:, gcol * KC + c:gcol * KC + c + 1],
                in1=RS[:, 0:tot], op0=ALU.mult, op1=ALU.mult),
                hk + [('rs', hf) for hf in range(nh)],
                [('xn', tag, hf, c) for hf in range(nh)] + (hk if dst is src else []))
            if store and c % 4 == 3:
                ov = outT.rearrange("(c p) t -> p c t", p=128)
                c0 = c - 3
                load('sp', ov[:, c0:c + 1, :], dst[:, c0:c + 1, :], ('k', 'out', c0), [('out', c0)],
                     reads=[('xn', tag, hf, cc) for hf in range(nh) for cc in range(c0, c + 1)]
                     + [('h', tag, cc, hf) for hf in range(nh) for cc in range(c0, c + 1)])

    deferred = []
    bg = []
    bg_low = []
    bg_rate = [1]

    def defer(fn, delay=1):
        deferred.append([delay, fn])

    def hook():
        run = []
        keep = []
        for e in deferred:
            e[0] -= 1
            (run if e[0] <= 0 else keep).append(e)
        deferred[:] = keep
        for e in run:
            e[1]()
        for _ in range(bg_rate[0]):
            if bg:
                bg.pop(0)()
        if bg_low:
            bg_low.pop(0)()

    def drain_all():
        while deferred or bg or bg_low:
            run = deferred[:]
            del deferred[:]
            for e in run:
                e[1]()
            if bg:
                bg.pop(0)()
            elif bg_low:
                bg_low.pop(0)()

    def xnk(tag='main'):
        return lambda hf: [('xn', tag, hf, c) for c in range(KC)]

    def proj_fm(Wr, cb, nk, rhs_fn, widths, evac_fn, rhs_reads, cls='A'):
        slot = wfm_i[0] % NWFM
        wfm_i[0] += 1
        load('pool', WFM[slot][:, 0:nk, :], Wr[cb].rearrange("p (k c) -> p k c", c=128)[:, 0:nk, :], ('wfm', slot), [('wfm', slot)])
        for hf, w in enumerate(widths):
            bk = bank(cls)
            rr = rhs_reads(hf) if callable(rhs_reads) else rhs_reads
            for k in range(nk):
                mm(PS[bk][:, 0:w], WFM[slot][:, k, :], rhs_fn(k, hf), k == 0, k == nk - 1,
                   [('wfm', slot)] + rr, bk)
            hook()
            evac_fn(hf, bk)

    def ffn(Wgu, Wd, gcol, banks_in, full_stats):
        with ExitStack() as st:
            if full_stats:
                norm_stats(HB, NT, 'main', banks_in)
            norm_apply(HB, NT, gcol, XN, 'main', banks_in)
            ACTB = [sb("actb%d" % i, [128, NT], BF16, st) for i in range(2 * G)]
            WD = [sb("wd%d" % i, [128, D], BF16, st) for i in range(2 * G)]
            SG = [sb("sg%d" % i, [128, 512], F32, st) for i in range(2)]
            sgi = [0]
            for gi in range(NF // G):
                for j in range(G):
                    f = gi * G + j
                    s = (gi % 2) * G + j
                    load('pool', WD[s][:], Wd[f * 128:(f + 1) * 128, :], ('wd', s), [('wd', s)])
                for j in range(G):
                    f = gi * G + j
                    s = (gi % 2) * G + j
                    gbanks = {}

                    def ev_gate(hf, bk):
                        gbanks[hf] = bk

                    proj_fm(Wgu, 2 * f, KC, lambda k, hf: XN[:, k, hf * 512:(hf + 1) * 512], [512, 512],
                            ev_gate, xnk(), 'A')

                    def ev_up(hf, bk, s=s):
                        q = SG[sgi[0] % 2]
                        qi = sgi[0] % 2
                        sgi[0] += 1
                        gb = gbanks[hf]
                        act(lambda: nc.scalar.activation(out=q[:], in_=PS[gb][:], func=AF.Silu),
                            [('ps', gb)], [('sg', qi)])
                        dve(lambda: nc.vector.tensor_tensor(out=ACTB[s][:, hf * 512:(hf + 1) * 512], in0=q[:],
                                                            in1=PS[bk][:], op=ALU.mult),
                            [('sg', qi), ('ps', bk)], [('actb', s)])

                    proj_fm(Wgu, 2 * f + 1, KC, lambda k, hf: XN[:, k, hf * 512:(hf + 1) * 512], [512, 512],
                            ev_up, xnk(), 'B')
                for oc in range(KC):
                    for hf in range(2):
                        bk = bank('C')
                        for j in range(G):
                            s = (gi % 2) * G + j
                            mm(PS[bk][:], WD[s][:, oc * 128:(oc + 1) * 128], ACTB[s][:, hf * 512:(hf + 1) * 512],
                               j == 0, j == G - 1, [('wd', s), ('actb', s)], bk)
                        sl = slice(hf * 512, (hf + 1) * 512)
                        dve(lambda oc=oc, sl=sl, bk=bk: nc.vector.scalar_tensor_tensor(
                            out=HB[:, oc, sl], in0=PS[bk][:], scalar=0.5, in1=HB[:, oc, sl],
                            op0=ALU.mult, op1=ALU.add),
                            [('ps', bk), ('h', 'main', oc, hf)], [('h', 'main', oc, hf)])
                        hook()
                        if gi == NF // G - 1:
                            stat_acc(HB, oc, hf, 512, 'main', (6, 7), 2)
            drain_all()
            T.barrier()

    def cs_temps(st):
        return (sb("posi", [128, NT], I32, st), sb("ang", [128, NT], F32, st), sb("kf", [128, NT], F32, st),
                sb("ki", [128, NT], I32, st), sb("mk", [128, NT], F32, st))

    def make_cs(pos_ap, COS, SIN, temps, tag):
        PI_, A, Kf, Ki, Mk = temps
        load('sp', PI_[:], pos_ap, ('k', 'pos' + tag), ['posi'])
        dve(lambda: nc.vector.tensor_copy(out=A[:], in_=PI_[:]), ['posi'], ['ang'])
        dve(lambda: nc.vector.tensor_scalar(out=A[:], in0=A[:], scalar1=CST[:, 0:1], scalar2=None, op0=ALU.mult),
            ['ang'], ['ang'])
        for which, R in ((0, SIN), (1, COS)):
            off = 0.0 if which == 0 else TWO_PI / 4
            rk = ('cs', tag, which)
            dve(lambda off=off: nc.vector.tensor_scalar(out=Kf[:], in0=A[:], scalar1=off, scalar2=1.0 / TWO_PI,
                                                        op0=ALU.add, op1=ALU.mult), ['ang'], ['kf'])
            dve(lambda: nc.vector.tensor_copy(out=Ki[:], in_=Kf[:]), ['kf'], ['ki'])
            dve(lambda: nc.vector.tensor_copy(out=Kf[:], in_=Ki[:]), ['ki'], ['kf'])
            dve(lambda off=off, R=R: nc.vector.tensor_scalar(out=R[:], in0=A[:], scalar1=off, scalar2=None, op0=ALU.add),
                ['ang'], [rk])
            dve(lambda R=R: nc.vector.scalar_tensor_tensor(out=R[:], in0=Kf[:], scalar=-C1, in1=R[:],
                                                           op0=ALU.mult, op1=ALU.add), ['kf', rk], [rk])
            dve(lambda R=R: nc.vector.scalar_tensor_tensor(out=R[:], in0=Kf[:], scalar=-C2, in1=R[:],
                                                           op0=ALU.mult, op1=ALU.add), ['kf', rk], [rk])
            dve(lambda R=R: nc.vector.tensor_scalar(out=Mk[:], in0=R[:], scalar1=3.141592, scalar2=None, op0=ALU.is_gt),
                [rk], ['mk'])
            dve(lambda R=R: nc.vector.scalar_tensor_tensor(out=R[:], in0=Mk[:], scalar=-TWO_PI, in1=R[:],
                                                           op0=ALU.mult, op1=ALU.add), ['mk', rk], [rk])
            dve(lambda R=R: nc.vector.tensor_scalar(out=Mk[:], in0=R[:], scalar1=-3.141592, scalar2=None, op0=ALU.is_lt),
                [rk], ['mk'])
            dve(lambda R=R: nc.vector.scalar_tensor_tensor(out=R[:], in0=Mk[:], scalar=TWO_PI, in1=R[:],
                                                           op0=ALU.mult, op1=ALU.add), ['mk', rk], [rk])
            dve(lambda R=R: nc.vector.tensor_scalar(out=R[:], in0=R[:], scalar1=3.1415925, scalar2=-3.1415925,
                                                    op0=ALU.min, op1=ALU.max), [rk], [rk])

        def fin():
            for which, R in ((0, SIN), (1, COS)):
                rk = ('cs', tag, which)
                act(lambda R=R: nc.scalar.activation(out=R[:], in_=R[:], func=AF.Sin), [rk], [rk])
            dve(lambda: nc.vector.tensor_scalar(out=SIN[:], in0=SIN[:], scalar1=CST[:, 1:2], scalar2=None, op0=ALU.mult),
                [('cs', tag, 0)], [('cs', tag, 0)])
        return fin

    def make_rope(COS, SIN, st, tag):
        QF = [sb("qf%s%d" % (tag, i), [128, 512], F32, st) for i in range(2)]
        T1 = [sb("t1%s" % tag, [128, 512], F32, st)] * 2
        T2 = [sb("t2%s" % tag, [128, 512], F32, st)] * 2
        ri = [0]

        def rope(bk, hf, dst, dst_key, c0=None, w=512):
            i = ri[0] % 2
            ri[0] += 1
            if c0 is None:
                c0 = hf * 512
            sl = slice(c0, c0 + w)
            act(lambda: nc.scalar.copy(out=QF[i][:, 0:w], in_=PS[bk][:, 0:w]), [('ps', bk)], [('qf', i)])

            def stage2():
                b2 = bank('M')
                mm(PS[b2][:, 0:w], SWP[:], QF[i][:, 0:w], True, True, [('qf', i)], b2)
                dve(lambda: nc.vector.tensor_tensor(out=T1[i][:, 0:w], in0=QF[i][:, 0:w], in1=COS[:, sl], op=ALU.mult),
                    [('qf', i), ('cs', tag, 1)], [('t1', 0)])
                dve(lambda: nc.vector.tensor_tensor(out=T2[i][:, 0:w], in0=PS[b2][:, 0:w], in1=SIN[:, sl], op=ALU.mult),
                    [('ps', b2), ('cs', tag, 0)], [('t2', 0)])
                dve(lambda: nc.vector.tensor_tensor(out=dst, in0=T1[i][:, 0:w], in1=T2[i][:, 0:w], op=ALU.add),
                    [('t1', 0), ('t2', 0)], [dst_key])
            defer(stage2, 1)
        return rope

    def tile_ap(buf2d, g, t):
        if g == 0:
            return buf2d[:, t * 128:(t + 1) * 128]
        if g == 1:
            r, lb = t // 2, t % 2
            return buf2d[:, 512 * lb + r:512 * lb + 512:4]
        return buf2d[:, t:NT:16]

    def v_tiles(VT, g, dst, dst_key, vt_key):
        PSB3 = PSB[:, :].rearrange("p (t d) -> p t d", d=128)
        vk = list(vt_key) if isinstance(vt_key, list) else [vt_key]
        if g < 2:
            for t in range(8):
                T.op('pe', lambda t=t: nc.tensor.transpose(PSB[:, t * 128:(t + 1) * 128], tile_ap(VT, g, t), IDB[:]),
                     vk + ['idb'], [('ps', PSBK)])
            act(lambda: nc.scalar.copy(out=dst[:, 0:8, :], in_=PSB3), [('ps', PSBK)], [dst_key])
        else:
            for rb in range(2):
                for r8 in range(8):
                    T.op('pe', lambda r8=r8, rb=rb: nc.tensor.transpose(PSB[0:64, r8 * 128:(r8 + 1) * 128],
                                                                    tile_ap(VT, 2, rb * 8 + r8), IDB[:]),
                         vk + ['idb'], [('ps', PSBK)])
                act(lambda rb=rb: nc.scalar.copy(out=dst[0:64, rb * 8:(rb + 1) * 8, :], in_=PSB3[0:64]),
                    [('ps', PSBK)], [dst_key])

    def load_x(src, nm):
        v = src.rearrange("(c p) t -> p c t", p=128)
        for q in range(4):
            load('sp', HB[:, 4 * q:4 * q + 4, :], v[:, 4 * q:4 * q + 4, :], ('k', nm, q),
                 [('h', 'main', c, hf) for c in range(4 * q, 4 * q + 4) for hf in range(2)])

    load_x(xT_oth, 'xo')
    ffn(w1gu, w1d, 0, (6, 7), True)
    with ExitStack() as st:
        norm_apply(HB, NT, 1, XN, 'main', (6, 7))
        load_x(xT_own, 'xw')
        COS = sb("cos1", [128, NT], F32, st)
        SIN = sb("sin1", [128, NT], F32, st)
        temps = cs_temps(st)
        cs_fin = [make_cs(pos_oth, COS, SIN, temps, 'o'), make_cs(pos_own, COS2, SIN2, temps, 'w')]
        rope = make_rope(COS, SIN, st, 'o')
        KST = [sb("kst%d" % i, [128, NT], BF16, st) for i in range(2)]
        VTS = [sb("vts%d" % i, [128, NT], BF16, st) for i in range(2)]
        VST = [sb("vst%d" % i, [128, 16, 128], BF16, st) for i in range(2)]
        xr = lambda k, hf: XN[:, k, hf * 512:(hf + 1) * 512]
        bg_rate[0] = 1
        G0C = (0, 896)
        for hd in range(12):
            g = hd // 4
            i = hd % 2

            if g == 0:
                def ev_v(hf, bk, i=i):
                    c0 = G0C[hf]
                    act(lambda: nc.scalar.copy(out=VTS[i][:, c0:c0 + 128], in_=PS[bk][:, 0:128]),
                        [('ps', bk)], [('vts', i, hf)])
                proj_fm(win1, 2 * hd + 1, KC, lambda k, hf: XN[:, k, G0C[hf]:G0C[hf] + 128], [128, 128], ev_v, xnk(), 'B')
            else:
                def ev_v(hf, bk, i=i):
                    act(lambda: nc.scalar.copy(out=VTS[i][:, hf * 512:(hf + 1) * 512], in_=PS[bk][:]),
                        [('ps', bk)], [('vts', i, hf)])
                proj_fm(win1, 2 * hd + 1, KC, xr, [512, 512], ev_v, xnk(), 'B')

            def vt_bg(hd=hd, g=g, i=i):
                if g == 0:
                    for idx, t in enumerate((0, 7)):
                        T.op('pe', lambda idx=idx, t=t: nc.tensor.transpose(PSB[:, idx * 128:(idx + 1) * 128],
                                                                        VTS[i][:, t * 128:(t + 1) * 128], IDB[:]),
                             [('vts', i, 0), ('vts', i, 1), 'idb'], [('ps', PSBK)])
                    act(lambda: nc.scalar.copy(out=VST[i][:, 0:8:7, :],
                                               in_=PSB[:, 0:256].rearrange("p (t d) -> p t d", d=128)),
                        [('ps', PSBK)], [('vst', i)])
                else:
                    v_tiles(VTS[i][:], g, VST[i], ('vst', i), [('vts', i, 0), ('vts', i, 1)])
                if g < 2:
                    load('sp', scr_v[hd][:, 0:8, :], VST[i][:, 0:8, :], ('st', 'v', i), [('scrv', hd)], reads=[('vst', i)])
                else:
                    load('sp', scr_v[hd][0:64], VST[i][0:64], ('st', 'v', i), [('scrv', hd)], reads=[('vst', i)])
            bg.append(vt_bg)
            if hd == 6:
                for f in cs_fin:
                    f()
            if hd == 5:
                for hf in range(2):
                    for c in range(KC):
                        bg_low.append(lambda c=c, hf=hf: stat_acc(HB, c, hf, 512, 'main', (4, 5), 1))
        for hd in range(12):
            i = hd % 2
            if hd < 4:
                proj_fm(win1, 2 * hd, KC, lambda k, hf: XN[:, k, G0C[hf]:G0C[hf] + 128], [128, 128],
                        lambda hf, bk, i=i: rope(bk, hf, KST[i][:, G0C[hf]:G0C[hf] + 128], ('kst', i, hf),
                                                 c0=G0C[hf], w=128), xnk(), 'A')
            else:
                proj_fm(win1, 2 * hd, KC, xr, [512, 512],
                        lambda hf, bk, i=i: rope(bk, hf, KST[i][:, hf * 512:(hf + 1) * 512], ('kst', i, hf)),
                        xnk(), 'A')
            defer(lambda hd=hd, i=i: load('sp', scr_k[hd], KST[i][:], ('st', 'k', i), [('scrk', hd)],
                                          reads=[('kst', i, 0), ('kst', i, 1)]), 2)
        for gpl in range(4):
            def ev_h(hf, bk, gpl=gpl):
                dve(lambda: nc.vector.tensor_scalar(out=UPH[:, gpl, hf * 8:(hf + 1) * 8], in0=PS[bk][:, 0:8],
                                                    scalar1=CST[:, 2 + hf:3 + hf], scalar2=None, op0=ALU.mult),
                    [('ps', bk)], [('uph', gpl, hf)])
            proj_fm(win1, 24 + gpl, KC,
                    lambda k, hf: XN[:, k, 1016:1024] if hf == 0 else XN[:, k, 0:8], [8, 8], ev_h,
                    lambda hf: [('xn', 'main', 1 - hf, c) for c in range(KC)], 'A')
        drain_all()
        T.barrier()

    ffn(w1gu, w1d, 0, (4, 5), False)
    with ExitStack() as st:
        norm_apply(HB, NT, 1, XN, 'main', (6, 7))
        CCa = sb("cca", [128, 4, NT], BF16, st)
        with ExitStack() as sta:
            rope = make_rope(COS2, SIN2, sta, 'w')
            QG = sb("qg", [128, 3, NT], BF16, sta)
            KG = sb("kg", [128, 3, 2 * NT], BF16, sta)
            VG = sb("vg", [128, 3, 16, 128], BF16, sta)
            VTS = sb("vts2", [128, NT], BF16, sta)
            NUM = sb("num", [128, NT], F32, sta)
            DEN = sb("den", [128, NT], F32, sta)
            PT = [sb("pt%d" % i, [128, 512], BF16, sta) for i in range(2)]
            PE_ = [sb("pe%d" % i, [128, 512], BF16, sta) for i in range(2)]
            xr = lambda k, hf: XN[:, k, hf * 512:(hf + 1) * 512]
            pti = [0]
            SC = 128 ** -0.5
            NDK = lambda nm, blks: [(nm, bb) for bb in blks]

            def att_closures(j, g):
                tiles = list(range(8 if g < 2 else 2))
                info = {}

                def A(t):
                    if g == 0:
                        kts = [(t - 1) % 16, t, t + 1]
                        kaps = [KG[:, g, kt * 128:(kt + 1) * 128] for kt in kts]
                        vaps = [VG[:, g, kt if kt < 8 else (8 if kt == 8 else 15), :] for kt in kts]
                        mcol = 384 if t == 0 else (768 if t == 7 else 0)
                        blks = [t]
                    elif g == 1:
                        r, lb = t // 2, t % 2
                        lts = [(lb - 1) % 4, lb, lb + 1]
                        kaps = [KG[:, g, 512 * lt + r:512 * lt + 512:4] for lt in lts]
                        vaps = [VG[:, g, (r * 2 + lt) if lt < 2 else (8 + r * 2 + (lt - 2)), :] for lt in lts]
                        mcol = 384 if lb == 0 else 768
                        blks = [4 * lb + x for x in range(4)]
                    else:
                        kaps = [KG[:, g, (t * 8 + r8):2 * NT:16] for r8 in range(8)]
                        vaps = [VG[:, g, t * 8 + r8, :] for r8 in range(8)]
                        mcol = 1152
                        blks = list(range(8))
                    nk = len(kaps)
                    qw = 128 if g < 2 else 64
                    w = nk * qw
                    bk = bank('C')
                    for ki, kap in enumerate(kaps):
                        qap = tile_ap(QG[:, g, :], g, t) if g < 2 else tile_ap(QG[:, g, :], 2, t * 8 + ki)
                        mm(PS[bk][:, ki * qw:(ki + 1) * qw], kap, qap, True, True,
                           [('kg', g, 0), ('kg', g, 1), ('kgo', g), ('qg', g, 0), ('qg', g, 1)], bk)
                    pi = pti[0] % 2
                    pti[0] += 1
                    act(lambda: nc.scalar.activation(out=PE_[pi][:, 0:w], in_=PS[bk][:, 0:w], func=AF.Exp, scale=SC),
                        [('ps', bk)], [('pe', pi)])
                    dve(lambda: nc.vector.tensor_tensor(out=PT[pi][:, 0:w], in0=PE_[pi][:, 0:w],
                                                        in1=MSK[:, mcol:mcol + w], op=ALU.mult),
                        [('pe', pi), 'msk'], [('pt', pi)])
                    info[t] = (pi, nk, vaps, blks)

                def B(t):
                    pi, nk, vaps, blks = info[t]
                    bo, bd = 2, 3
                    if g < 2:
                        for ki, vap in enumerate(vaps):
                            mm(PS[bo][:, 0:128], vap, PT[pi][:, ki * 128:(ki + 1) * 128], ki == 0, ki == nk - 1,
                               [('vg', g), ('vgo', g), ('pt', pi)], bo)
                        for ki in range(nk):
                            mm(PS[bd][:, 0:128], ONB[:], PT[pi][:, ki * 128:(ki + 1) * 128], ki == 0, ki == nk - 1,
                               [('pt', pi), 'onb'], bd)
                        nap = tile_ap(NUM[:], g, t)
                        dap = tile_ap(DEN[:], g, t)
                        pso = PS[bo][:, 0:128]
                        psd = PS[bd][:, 0:128]
                    else:
                        for ki, vap in enumerate(vaps):
                            mm(PS[bo][:, ki * 64:(ki + 1) * 64], vap, PT[pi][:, ki * 64:(ki + 1) * 64], True, True,
                               [('vg', g), ('vgo', g), ('pt', pi)], bo)
                        for ki in range(nk):
                            mm(PS[bd][:, ki * 64:(ki + 1) * 64], ONB[:], PT[pi][:, ki * 64:(ki + 1) * 64], True, True,
                               [('pt', pi), 'onb'], bd)
                        nap = NUM[:].rearrange("p (l s) -> p l s", s=16)[:, :, t * 8:(t + 1) * 8]
                        dap = DEN[:].rearrange("p (l s) -> p l s", s=16)[:, :, t * 8:(t + 1) * 8]
                        pso = PS[bo][:, :].rearrange("p (s l) -> p l s", s=8)
                        psd = PS[bd][:, :].rearrange("p (s l) -> p l s", s=8)
                    nk_, dk_ = NDK('num', blks), NDK('den', blks)
                    if g == 0:
                        act(lambda: nc.scalar.copy(out=nap, in_=pso), [('ps', bo)], nk_)
                        dve(lambda: nc.vector.tensor_copy(out=dap, in_=psd), [('ps', bd)], dk_)
                    else:
                        dve(lambda: nc.vector.tensor_tensor(out=nap, in0=pso, in1=nap, op=ALU.add),
                            [('ps', bo)] + nk_, nk_)
                        dve(lambda: nc.vector.tensor_tensor(out=dap, in0=psd, in1=dap, op=ALU.add),
                            [('ps', bd)] + dk_, dk_)

                cl = []
                for idx, t in enumerate(tiles):
                    cl.append(lambda t=t: A(t))
                    if idx >= 1:
                        cl.append(lambda t=tiles[idx - 1]: B(t))
                cl.append(lambda t=tiles[-1]: B(t))
                if g == 2:
                    def fin():
                        allb = list(range(8))
                        dve(lambda: nc.vector.reciprocal(out=DEN[:], in_=DEN[:]), NDK('den', allb), NDK('den', allb))
                        dve(lambda: nc.vector.tensor_tensor(out=CCa[:, j, :], in0=NUM[:], in1=DEN[:], op=ALU.mult),
                            NDK('num', allb) + NDK('den', allb), ['cca'] + NDK('num', allb) + NDK('den', allb))
                    cl.append(fin)
                return cl

            for j in range(4):
                for g in range(3):
                    hd = 4 * g + j
                    base = (j * 3 + g) * 3
                    bg_rate[0] = max(1, (len(bg) + 5) // 6)
                    proj_fm(win2, base, KC, xr, [512, 512],
                            lambda hf, bk, g=g: rope(bk, hf, QG[:, g, hf * 512:(hf + 1) * 512], ('qg', g, hf)),
                            xnk(), 'A')
                    proj_fm(win2, base + 1, KC, xr, [512, 512],
                            lambda hf, bk, g=g: rope(bk, hf, KG[:, g, hf * 512:(hf + 1) * 512], ('kg', g, hf)),
                            xnk(), 'A')
                    load('sp', KG[:, g, NT:2 * NT], scr_k[hd], ('kgo', g), [('kgo', g)], reads=[('scrk', hd)])

                    def ev_v(hf, bk):
                        act(lambda: nc.scalar.copy(out=VTS[:, hf * 512:(hf + 1) * 512], in_=PS[bk][:]),
                            [('ps', bk)], [('vts2', hf)])
                    proj_fm(win2, base + 2, KC, xr, [512, 512], ev_v, xnk(), 'A')
                    drain_all()
                    v_tiles(VTS[:], g, VG[:, g], ('vg', g), [('vts2', 0), ('vts2', 1)])
                    if g < 2:
                        load('sp', VG[:, g, 8:16, :], scr_v[hd][:, 0:8, :], ('vgo', g), [('vgo', g)], reads=[('scrv', hd)])
                    else:
                        load('sp', VG[64:128, g, :, :], scr_v[hd][0:64], ('vgo', g), [('vgo', g)], reads=[('scrv', hd)])
                    bg.extend(att_closures(j, g))
            drain_all()
            T.barrier()
        with ExitStack() as stp:
            bg_rate[0] = 1
            CCp = sb("ccp", [128, 4, NT], BF16, stp)
            UPs = [sb("up%d" % i, [128, NT + 16], F32, stp) for i in range(2)]
            A1 = sb("a1", [128, NT + 16], F32, stp)
            B4 = sb("b4", [128, NT + 16], F32, stp)
            PW = sb("pw", [128, 4, 128], BF16, stp)
            PLs = [sb("pl%d" % i, [128, NT], BF16, stp) for i in range(2)]
            load('pool', PW[:], poolw.rearrange("g c d -> c g d"), ('k', 'pw'), ['pw'])
            xr = lambda k, hf: XN[:, k, hf * 512:(hf + 1) * 512]
            W_ = NT + 16
            for gpl in range(4):
                wlen = (2, 4, 8, 16)[gpl]
                UP = UPs[gpl % 2]
                PL = PLs[gpl % 2]
                upk = ('up', gpl % 2)
                plk = ('pl', gpl % 2)

                def ev_u(hf, bk, UP=UP, upk=upk):
                    act(lambda: nc.scalar.copy(out=UP[:, 8 + hf * 512:8 + (hf + 1) * 512], in_=PS[bk][:]),
                        [('ps', bk)], [upk])
                proj_fm(win2, 36 + gpl, KC, xr, [512, 512], ev_u, xnk(), 'A')
                dve(lambda gpl=gpl, UP=UP: nc.vector.tensor_copy(out=UP[:, 0:8], in_=UPH[:, gpl, 0:8]), [], [upk])
                dve(lambda gpl=gpl, UP=UP: nc.vector.tensor_copy(out=UP[:, NT + 8:NT + 16], in_=UPH[:, gpl, 8:16]),
                    [], [upk])
                dve(lambda UP=UP: nc.vector.tensor_tensor(out=A1[:, 1:W_], in0=UP[:, 1:W_], in1=UP[:, 0:W_ - 1],
                                                          op=ALU.add), [upk], ['a1'])
                cur, curk = A1, 'a1'
                if gpl >= 1:
                    dve(lambda: nc.vector.tensor_tensor(out=B4[:, 2:W_ - 1], in0=A1[:, 1:W_ - 2], in1=A1[:, 3:W_],
                                                        op=ALU.add), ['a1'], ['b4'])
                    cur, curk = B4, 'b4'
                if gpl >= 2:
                    dve(lambda: nc.vector.tensor_tensor(out=A1[:, 4:W_ - 3], in0=B4[:, 2:W_ - 5], in1=B4[:, 6:W_ - 1],
                                                        op=ALU.add), ['b4'], ['a1'])
                    cur, curk = A1, 'a1'
                if gpl >= 3:
                    dve(lambda: nc.vector.tensor_tensor(out=B4[:, 8:W_ - 7], in0=A1[:, 4:W_ - 11], in1=A1[:, 12:W_ - 3],
                                                        op=ALU.add), ['a1'], ['b4'])
                    cur, curk = B4, 'b4'
                dve(lambda cur=cur, gpl=gpl: nc.vector.tensor_tensor(out=cur[:, 8:16], in0=cur[:, 8:16],
                                                                     in1=EDGE[:, gpl * 16:gpl * 16 + 8], op=ALU.mult),
                    [curk], [curk])
                dve(lambda cur=cur, gpl=gpl: nc.vector.tensor_tensor(out=cur[:, NT:NT + 8], in0=cur[:, NT:NT + 8],
                                                                     in1=EDGE[:, gpl * 16 + 8:gpl * 16 + 16], op=ALU.mult),
                    [curk], [curk])
                dve(lambda cur=cur, wlen=wlen, UP=UP, PL=PL: nc.vector.scalar_tensor_tensor(
                    out=PL[:], in0=cur[:, 8:NT + 8], scalar=1.0 / wlen, in1=UP[:, 8:NT + 8],
                    op0=ALU.mult, op1=ALU.subtract), [curk, upk], [plk])
                for hf in range(2):
                    bk = bank('B')
                    mm(PS[bk][:], PW[:, gpl, :], PL[:, hf * 512:(hf + 1) * 512], True, True, ['pw', plk], bk)
                    act(lambda gpl=gpl, hf=hf, bk=bk: nc.scalar.activation(
                        out=CCp[:, gpl, hf * 512:(hf + 1) * 512], in_=PS[bk][:], func=AF.Copy,
                        scale=PSC[:, gpl:gpl + 1]), [('ps', bk)], ['ccp'])
            for oc in range(KC):
                def ev_o(hf, bk, oc=oc):
                    sl = slice(hf * 512, (hf + 1) * 512)
                    dve(lambda: nc.vector.tensor_tensor(out=HB[:, oc, sl], in0=PS[bk][:], in1=HB[:, oc, sl], op=ALU.add),
                        [('ps', bk), ('h', 'main', oc, hf)], [('h', 'main', oc, hf)])
                    stat_acc(HB, oc, hf, 512, 'main', (6, 7), 2)
                proj_fm(wout, oc, 8, lambda k, hf: (CCp[:, k, hf * 512:(hf + 1) * 512] if k < 4
                                                    else CCa[:, k - 4, hf * 512:(hf + 1) * 512]),
                        [512, 512], ev_o, ['ccp', 'cca'], 'W')
            drain_all()
            T.barrier()
    T.barrier()

    with ExitStack() as st:
        MN = sb("mn", [128, KC, 256], BF16, st)
        KCm = sb("kcm", [128, KC, 256], BF16, st)
        VCm = sb("vcm", [128, 2, D], BF16, st)
        with ExitStack() as st2:
            MF = sb("mf", [128, KC, 256], F32, st2)
            load('sp', MF[:], memT.rearrange("(c p) t -> p c t", p=128), ('k', 'mem'),
                 [('h', 'mem', c, 0) for c in range(KC)])
            norm_stats(MF, 256, 'mem', (4, 4))
            norm_apply(MF, 256, 3, MN, 'mem', (4, 4))
            T.barrier()
        norm_apply(HB, NT, 2, XN, 'main', (6, 7))
        bg_rate[0] = 1
        mr = lambda k, hf: MN[:, k, :]
        VT2 = [sb("vt2%d" % i, [128, 256], BF16, st) for i in range(2)]
        for cb in range(KC):
            proj_fm(wck, cb, KC, mr, [256],
                    lambda hf, bk, cb=cb: act(lambda: nc.scalar.copy(out=KCm[:, cb, :], in_=PS[bk][:, 0:256]),
                                              [('ps', bk)], ['kcm']), [('xn', 'mem', 0, c) for c in range(KC)], 'A')
        for cb in range(KC):
            i = cb % 2

            def ev_cv(hf, bk, i=i, cb=cb):
                act(lambda: nc.scalar.copy(out=VT2[i][:], in_=PS[bk][:, 0:256]), [('ps', bk)], [('vt2', i)])

                def st2_():
                    for mt in range(2):
                        T.op('pe', lambda mt=mt: nc.tensor.transpose(PSB[:, mt * 128:(mt + 1) * 128],
                                                                   VT2[i][:, mt * 128:(mt + 1) * 128], IDB[:]),
                             [('vt2', i), 'idb'], [('ps', PSBK)])
                    act(lambda: nc.scalar.copy(out=VCm[:, :, cb * 128:(cb + 1) * 128],
                                               in_=PSB[:, 0:256].rearrange("p (m d) -> p m d", d=128)),
                        [('ps', PSBK)], ['vcm'])
                defer(st2_, 1)
            proj_fm(wcv, cb, KC, mr, [256], ev_cv, [('xn', 'mem', 0, c) for c in range(KC)], 'B')
        drain_all()
        QC = [sb("qc%d" % i, [128, 4, NT], BF16, st) for i in range(2)]
        OCh = [sb("och%d" % i, [128, 4, NT], BF16, st) for i in range(2)]
        PTc = [sb("ptc%d" % i, [128, 2, 512], BF16, st) for i in range(2)]
        RD = [sb("rd%d" % i, [128, 512], F32, st) for i in range(2)]
        xr = lambda k, hf: XN[:, k, hf * 512:(hf + 1) * 512]
        SCc = 512 ** -0.5
        ci = [0]

        def catt_closures(hh):
            qb = QC[hh % 2]
            ob = OCh[hh % 2]
            info = {}

            def S1(hf):
                sl = slice(hf * 512, (hf + 1) * 512)
                pi = ci[0] % 2
                ci[0] += 1
                for mt in range(2):
                    bk = bank('C')
                    for dc in range(4):
                        mm(PS[bk][:], KCm[:, 4 * hh + dc, mt * 128:(mt + 1) * 128], qb[:, dc, sl], dc == 0, dc == 3,
                           ['kcm', ('qc', hh % 2, hf)], bk)
                    act(lambda bk=bk, mt=mt: nc.scalar.activation(out=PTc[pi][:, mt, :], in_=PS[bk][:],
                                                                  func=AF.Exp, scale=SCc),
                        [('ps', bk)], [('ptc', pi)])
                info[hf] = pi

            def S2(hf):
                sl = slice(hf * 512, (hf + 1) * 512)
                pi = info[hf]
                bd = bank('M')
                for mt in range(2):
                    mm(PS[bd][:], ONB[:], PTc[pi][:, mt, :], mt == 0, mt == 1, [('ptc', pi), 'onb'], bd)
                dve(lambda: nc.vector.reciprocal(out=RD[pi][:], in_=PS[bd][:]), [('ps', bd)], [('rd', pi)])
                for dc in range(4):
                    bo = bank('B')
                    for mt in range(2):
                        mm(PS[bo][:], VCm[:, mt, (4 * hh + dc) * 128:(4 * hh + dc + 1) * 128], PTc[pi][:, mt, :],
                           mt == 0, mt == 1, ['vcm', ('ptc', pi)], bo)
                    dve(lambda dc=dc, bo=bo: nc.vector.tensor_tensor(out=ob[:, dc, sl], in0=PS[bo][:], in1=RD[pi][:],
                                                                     op=ALU.mult),
                        [('ps', bo), ('rd', pi)], [('och', hh % 2, hf)])
            return [lambda: S1(0), lambda: S1(1), lambda: S2(0), lambda: S2(1)]

        for hh in range(5):
            if hh >= 1:
                bg.extend(catt_closures(hh - 1))
            bg_rate[0] = 1
            if hh < 4:
                for dc in range(4):
                    proj_fm(wcq, 4 * hh + dc, KC, xr, [512, 512],
                            lambda hf, bk, dc=dc, hh=hh: act(
                                lambda: nc.scalar.copy(out=QC[hh % 2][:, dc, hf * 512:(hf + 1) * 512], in_=PS[bk][:]),
                                [('ps', bk)], [('qc', hh % 2, hf)]), xnk(), 'A')
            drain_all()
            if hh in (2, 4):
                pr = hh // 2 - 1
                for oc in range(KC):
                    def ev_co(hf, bk, oc=oc, pr=pr):
                        sl = slice(hf * 512, (hf + 1) * 512)
                        dve(lambda: nc.vector.tensor_tensor(out=HB[:, oc, sl], in0=PS[bk][:], in1=HB[:, oc, sl], op=ALU.add),
                            [('ps', bk), ('h', 'main', oc, hf)], [('h', 'main', oc, hf)])
                        if pr == 1:
                            stat_acc(HB, oc, hf, 512, 'main', (6, 7), 2)
                    proj_fm(wco, pr * 16 + oc, 8, lambda k, hf: OCh[k // 4][:, k % 4, hf * 512:(hf + 1) * 512],
                            [512, 512], ev_co, lambda hf: [('och', 0, hf), ('och', 1, hf)], 'W')
        drain_all()
        T.barrier()

    ffn(w2gu, w2d, 4, (6, 7), False)
    norm_apply(HB, NT, 5, HB, 'main', (6, 7), store=True)
    T.barrier(final=True)

    es.close()
    ses.close()
    return nc


def _fm(W, nk):
    K, C = W.shape
    nb = C // 128
    return np.ascontiguousarray(W.reshape(nk, 128, nb, 128).transpose(2, 1, 0, 3)).reshape(nb, 128, nk * 128)


_NC_CACHE = {}


def kernel(x, mem, positions, ffn1_norm, ffn1_w_gate, ffn1_w_up, ffn1_w_down, mix_norm, w_in, pool_w, pool_scale,
           w_out, cross_norm, mem_norm, w_cq, w_ck, w_cv, w_co, ffn2_norm, ffn2_w_gate, ffn2_w_up, ffn2_w_down,
           final_norm):
    f32 = np.float32
    x = np.asarray(x, f32)
    mem = np.asarray(mem, f32)
    positions = np.asarray(positions, np.int32)

    def gu(wg, wu):
        a = _fm(np.asarray(wg, f32)[0], KC)
        b = _fm(np.asarray(wu, f32)[0], KC)
        return np.ascontiguousarray(np.stack([a, b], axis=1).reshape(2 * NF, 128, KC * 128))

    w1gu = gu(ffn1_w_gate, ffn1_w_up)
    w2gu = gu(ffn2_w_gate, ffn2_w_up)
    w1d = np.ascontiguousarray(np.asarray(ffn1_w_down, f32)[0])
    w2d = np.ascontiguousarray(np.asarray(ffn2_w_down, f32)[0])
    winf = _fm(np.asarray(w_in, f32)[0], KC)
    i1 = []
    for hd in range(12):
        i1 += [16 + hd, 28 + hd]
    i1 += [0, 1, 2, 3]
    i2 = []
    for j in range(4):
        for g in range(3):
            hd = 4 * g + j
            i2 += [4 + hd, 16 + hd, 28 + hd]
    i2 += [0, 1, 2, 3]
    win1 = np.ascontiguousarray(winf[i1])
    win2 = np.ascontiguousarray(winf[i2])
    woutf = _fm(np.asarray(w_out, f32)[0], 8)
    wcqf = _fm(np.asarray(w_cq, f32)[0], KC)
    wckf = _fm(np.asarray(w_ck, f32)[0], KC)
    wcvf = _fm(np.asarray(w_cv, f32)[0], KC)
    wco0 = np.asarray(w_co, f32)[0]
    wcof = np.zeros((32, 128, 8 * 128), f32)
    for pr in range(2):
        wcof[pr * 16:(pr + 1) * 16] = _fm(wco0[pr * 1024:(pr + 1) * 1024, :], 8)
    poolw = np.ascontiguousarray(np.asarray(pool_w, f32)[0])

    def pc(v):
        return np.asarray(v, f32).reshape(KC, 128).T

    gains = np.ascontiguousarray(np.concatenate(
        [pc(ffn1_norm[0]), pc(mix_norm[0]), pc(cross_norm[0]), pc(mem_norm[0]), pc(ffn2_norm[0]), pc(final_norm)],
        axis=1))
    pscale = np.ascontiguousarray(np.asarray(pool_scale, f32)[0].reshape(4, 128).T)
    ident = np.eye(128, dtype=f32)
    swapm = np.zeros((128, 128), f32)
    for m in range(128):
        swapm[(m + 64) % 128, m] = 1.0
    ii = np.arange(128)[:, None]
    jj = np.arange(128)[None, :]
    m_prev = (ii >= jj + 64).astype(f32)
    m_own = (np.abs(ii - jj) <= 64).astype(f32)
    m_next = (ii <= jj - 64).astype(f32)
    z = np.zeros((128, 128), f32)
    li, ei = ii // 2, ii % 2
    lj, ej = jj // 2, jj % 2
    same = (ei == ej)
    invf = (10000.0 ** (-(np.arange(64, dtype=f32)) / 64.0)).astype(f32)

    in_maps = []
    for c in range(8):
        b, half = c // 2, c % 2
        own = slice(half * NT, (half + 1) * NT)
        oth = slice((1 - half) * NT, (2 - half) * NT)
        cst = np.zeros((128, 64), f32)
        cst[:, 0] = np.concatenate([invf, invf])
        cst[:64, 1] = -1.0
        cst[64:, 1] = 1.0
        cst[:, 2] = 1.0 if half == 1 else 0.0
        cst[:, 3] = 1.0 if half == 0 else 0.0
        edge = np.ones((128, 64), f32)
        for gi, w in enumerate((2, 4, 8, 16)):
            t = np.arange(NT) + half * NT
            lo = np.clip(t - w // 2, 0, 2048)
            hi = np.clip(t + w - w // 2, 0, 2048)
            corr = (w / (hi - lo)).astype(f32)
            edge[:, gi * 16:gi * 16 + 8] = corr[None, 0:8]
            edge[:, gi * 16 + 8:gi * 16 + 16] = corr[None, NT - 8:NT]
        m01 = np.concatenate([m_prev, m_own, m_next], axis=1)
        m01_first = np.concatenate([m_prev if half == 1 else z, m_own, m_next], axis=1)
        m01_last = np.concatenate([m_prev, m_own, m_next if half == 0 else z], axis=1)
        l2i = np.arange(128)[:, None]
        l2j = np.arange(64)[None, :]
        if half == 0:
            m2s = ((l2i < 64) | ((l2i - 64) <= l2j)).astype(f32)
        else:
            m2s = ((l2i < 64) | ((l2i - 64) >= l2j)).astype(f32)
        m2 = np.tile(m2s, (1, 8))
        masks = np.ascontiguousarray(np.concatenate([m01, m01_first, m01_last, m2, np.zeros((128, 256), f32)], axis=1))
        in_maps.append({
            "xT_own": np.ascontiguousarray(x[b, own, :].T),
            "xT_oth": np.ascontiguousarray(x[b, oth, :].T),
            "memT": np.ascontiguousarray(mem[b].T),
            "pos_own": np.ascontiguousarray(np.broadcast_to(positions[b, own][None, :], (128, NT))),
            "pos_oth": np.ascontiguousarray(np.broadcast_to(positions[b, oth][None, :], (128, NT))),
            "cst": cst, "edge": edge, "gains": gains, "pscale": pscale, "masks": masks,
            "ident": ident, "swapm": swapm,
            "w1gu": w1gu, "w1d": w1d, "w2gu": w2gu, "w2d": w2d, "win1": win1, "win2": win2,
            "poolw": poolw, "wout": woutf, "wcq": wcqf, "wck": wckf, "wcv": wcvf, "wco": wcof,
        })
    if 'nc' not in _NC_CACHE:
        _NC_CACHE['nc'] = build_nc()
    nc = _NC_CACHE['nc']
    res = run_bass_kernel_spmd(nc, in_maps, core_ids=list(range(8)))
    out = np.empty((4, 2048, D), f32)
    for c in range(8):
        b, half = c // 2, c % 2
        out[b, half * NT:(half + 1) * NT, :] = res.results[c]["outT"].T
    return out
```
